# Optimizing a Trainium2 kernel written in Bass

```python
import jax, jax.numpy as jnp
from jax import lax
import numpy as np

D_MODEL = 2048
BATCH = 4
SEQ = 2048
DEPTH = 1
DEC_BATCH = 128
DEC_SEQ = 1
PAST_LEN = 16384
PAGE_SIZE = 128

RWKV_HEADS = 16
HEAD_SIZE = 64
RWKV_WIDTH = RWKV_HEADS * HEAD_SIZE
LORA_RANK = 64
LRU_WIDTH = 1024
LRU_BLOCKS = 16
LRU_BLOCK = LRU_WIDTH // LRU_BLOCKS
CONV_W = 4
LRU_C = 8.0
SHIFT_W = 3 * RWKV_WIDTH + 2 * LORA_RANK
IN_W = SHIFT_W + RWKV_WIDTH + 2 * LRU_WIDTH + 2 * D_MODEL
RMS_EPS = 1e-6
GN_EPS = 1e-5 * HEAD_SIZE

kernel_name = "rwkv7_rglru_gated_parallel_decode_step"


def rmsnorm(x, g):
    xf = x.astype(jnp.float32)
    y = xf * lax.rsqrt(jnp.mean(xf * xf, axis=-1, keepdims=True) + RMS_EPS)
    return (y * g.astype(jnp.float32)).astype(x.dtype)


def wkv7_scan(r, decay, k, v, a_vec, b_vec, s0):
    def step(S, inp):
        r_t, d_t, k_t, v_t, a_t, b_t = inp
        sa = jnp.einsum('bhvk,bhk->bhv', S, a_t)
        S = S * d_t[:, :, None, :] + sa[..., None] * b_t[:, :, None, :] + v_t[..., None] * k_t[:, :, None, :]
        y = jnp.einsum('bhvk,bhk->bhv', S, r_t)
        return S, y
    xs = tuple(jnp.moveaxis(t.astype(jnp.float32), 1, 0) for t in (r, decay, k, v, a_vec, b_vec))
    s_last, ys = lax.scan(step, s0.astype(jnp.float32), xs)
    return jnp.moveaxis(ys, 0, 1), s_last


def rg_lru(xc, gate_x, gate_a, lam, h0):
    log_a = -LRU_C * gate_a * jax.nn.softplus(-lam)
    a = jnp.exp(log_a)
    mult = jnp.sqrt(-jnp.expm1(2.0 * log_a))
    b = mult * gate_x * xc
    b = b.at[:, 0].add(a[:, 0] * h0)

    def combine(lft, rgt):
        a_l, b_l = lft
        a_r, b_r = rgt
        return a_l * a_r, a_r * b_l + b_r

    _, h = lax.associative_scan(combine, (a, b), axis=1)
    return h, h[:, -1]


def mixer_layer(x, sh_prev, s_wkv, conv_buf, h_lru,
                norm_g, w_in, rwkv_mu, w_decay0, w_decay_up, w_iclr0, w_iclr_up, k_k, k_a, r_k,
                ln_x_g, ln_x_b, w_out_rwkv, conv_w, conv_b, lru_gx_w, lru_gx_b, lru_ga_w, lru_ga_b,
                lru_lambda, w_out_lru, w_out):
    B, T, _ = x.shape
    dt = x.dtype
    h = rmsnorm(x, norm_g)
    z = h @ w_in
    o1 = SHIFT_W
    o2 = o1 + RWKV_WIDTH
    o3 = o2 + LRU_WIDTH
    o4 = o3 + LRU_WIDTH
    z_sh, z_rg, z_lx, z_lg, z_m = z[..., :o1], z[..., o1:o2], z[..., o2:o3], z[..., o3:o4], z[..., o4:]

    z_prev = jnp.concatenate([sh_prev[:, None].astype(dt), z_sh[:, :-1]], axis=1)
    zs = z_sh + rwkv_mu * (z_prev - z_sh)
    C = RWKV_WIDTH
    r, k, v = zs[..., :C], zs[..., C:2 * C], zs[..., 2 * C:3 * C]
    zw, za = zs[..., 3 * C:3 * C + LORA_RANK], zs[..., 3 * C + LORA_RANK:]
    r = r.astype(jnp.float32)
    k = k.astype(jnp.float32)
    v = v.astype(jnp.float32)
    w = -jax.nn.softplus(-(w_decay0 + jnp.tanh(zw) @ w_decay_up).astype(jnp.float32)) - 0.5
    decay = jnp.exp(-jnp.exp(w))
    a = jax.nn.sigmoid((w_iclr0 + za @ w_iclr_up).astype(jnp.float32))
    hs = (B, T, RWKV_HEADS, HEAD_SIZE)
    kk = (k * k_k.astype(jnp.float32)).reshape(hs)
    kk = kk * lax.rsqrt(jnp.maximum(jnp.sum(kk * kk, axis=-1, keepdims=True), 1e-24))
    k = k * (1.0 + (a - 1.0) * k_a.astype(jnp.float32))
    r4, k4, v4, a4 = r.reshape(hs), k.reshape(hs), v.reshape(hs), a.reshape(hs)
    y, s_wkv_new = wkv7_scan(r4, decay.reshape(hs), k4, v4, -kk, kk * a4, s_wkv)
    mu = jnp.mean(y, axis=-1, keepdims=True)
    var = jnp.mean(jnp.square(y - mu), axis=-1, keepdims=True)
    y = (y - mu) * lax.rsqrt(var + GN_EPS)
    y = y * ln_x_g.reshape(RWKV_HEADS, HEAD_SIZE).astype(jnp.float32) + ln_x_b.reshape(RWKV_HEADS, HEAD_SIZE).astype(jnp.float32)
    y = y + jnp.sum(r4 * k4 * r_k.astype(jnp.float32), axis=-1, keepdims=True) * v4
    o_r = y.reshape(B, T, C).astype(dt) * jax.nn.silu(z_rg)
    y_r = o_r @ w_out_rwkv

    xpad = jnp.concatenate([conv_buf.astype(dt), z_lx], axis=1)
    xc = sum(conv_w[j] * xpad[:, j:j + T] for j in range(CONV_W)) + conv_b
    conv_new = xpad[:, -(CONV_W - 1):]
    xb = xc.reshape(B, T, LRU_BLOCKS, LRU_BLOCK)
    gx = jax.nn.sigmoid((jnp.einsum('btgi,gij->btgj', xb, lru_gx_w).reshape(B, T, LRU_WIDTH) + lru_gx_b).astype(jnp.float32))
    ga = jax.nn.sigmoid((jnp.einsum('btgi,gij->btgj', xb, lru_ga_w).reshape(B, T, LRU_WIDTH) + lru_ga_b).astype(jnp.float32))
    hseq, h_last = rg_lru(xc.astype(jnp.float32), gx, ga, lru_lambda.astype(jnp.float32), h_lru.astype(jnp.float32))
    o_g = hseq.astype(dt) * jax.nn.silu(z_lg)
    y_g = o_g @ w_out_lru

    m_r, m_g = z_m[..., :D_MODEL], z_m[..., D_MODEL:]
    merged = jax.nn.sigmoid(m_r) * y_r + jax.nn.sigmoid(m_g) * y_g
    out = x + merged @ w_out
    return out, z_sh[:, -1], s_wkv_new, conv_new, h_last


def setup_inputs(seed: int = 0) -> dict:
    key = jax.random.key(seed)
    ks = jax.random.split(key, 32)
    f32 = jnp.float32
    L = DEPTH

    def nrm(k, shape, scale):
        return jax.random.normal(k, shape, f32) * scale

    a_c = jax.random.uniform(ks[26], (L, LRU_WIDTH), f32, minval=0.9, maxval=0.999)
    a_base = a_c ** (1.0 / LRU_C)
    lru_lambda = jnp.log(a_base) - jnp.log1p(-a_base)
    return {
        "x_prompt": nrm(ks[0], (BATCH, SEQ, D_MODEL), 1.0),
        "x_sample": nrm(ks[1], (DEC_BATCH, DEC_SEQ, D_MODEL), 1.0),
        "state_shift": nrm(ks[2], (L, DEC_BATCH, SHIFT_W), 1.0),
        "state_wkv": nrm(ks[3], (L, DEC_BATCH, RWKV_HEADS, HEAD_SIZE, HEAD_SIZE), 0.3),
        "state_conv": nrm(ks[4], (L, DEC_BATCH, CONV_W - 1, LRU_WIDTH), 1.0),
        "state_lru": nrm(ks[5], (L, DEC_BATCH, LRU_WIDTH), 0.5),
        "norm_g": 1.0 + nrm(ks[6], (L, D_MODEL), 0.02),
        "w_in": nrm(ks[7], (L, D_MODEL, IN_W), D_MODEL ** -0.5),
        "rwkv_mu": jax.random.uniform(ks[8], (L, SHIFT_W), f32),
        "w_decay0": jax.random.uniform(ks[9], (L, RWKV_WIDTH), f32, minval=-5.0, maxval=-0.5),
        "w_decay_up": nrm(ks[10], (L, LORA_RANK, RWKV_WIDTH), 0.5 * LORA_RANK ** -0.5),
        "w_iclr0": nrm(ks[11], (L, RWKV_WIDTH), 0.3),
        "w_iclr_up": nrm(ks[12], (L, LORA_RANK, RWKV_WIDTH), LORA_RANK ** -0.5),
        "k_k": 0.85 + nrm(ks[13], (L, RWKV_WIDTH), 0.05),
        "k_a": 1.0 + nrm(ks[14], (L, RWKV_WIDTH), 0.05),
        "r_k": nrm(ks[15], (L, RWKV_HEADS, HEAD_SIZE), 0.1),
        "ln_x_g": 1.0 + nrm(ks[16], (L, RWKV_WIDTH), 0.02),
        "ln_x_b": nrm(ks[17], (L, RWKV_WIDTH), 0.02),
        "w_out_rwkv": nrm(ks[18], (L, RWKV_WIDTH, D_MODEL), RWKV_WIDTH ** -0.5),
        "conv_w": nrm(ks[19], (L, CONV_W, LRU_WIDTH), CONV_W ** -0.5),
        "conv_b": nrm(ks[20], (L, LRU_WIDTH), 0.02),
        "lru_gx_w": nrm(ks[21], (L, LRU_BLOCKS, LRU_BLOCK, LRU_BLOCK), LRU_BLOCK ** -0.5),
        "lru_gx_b": nrm(ks[22], (L, LRU_WIDTH), 0.02),
        "lru_ga_w": nrm(ks[23], (L, LRU_BLOCKS, LRU_BLOCK, LRU_BLOCK), LRU_BLOCK ** -0.5),
        "lru_ga_b": nrm(ks[24], (L, LRU_WIDTH), 0.02),
        "lru_lambda": lru_lambda,
        "w_out_lru": nrm(ks[25], (L, LRU_WIDTH, D_MODEL), LRU_WIDTH ** -0.5),
        "w_out": nrm(ks[27], (L, D_MODEL, D_MODEL), D_MODEL ** -0.5),
        "final_norm_g": 1.0 + nrm(ks[28], (D_MODEL,), 0.02),
    }


def reference(x_prompt, x_sample, state_shift, state_wkv, state_conv, state_lru,
              norm_g, w_in, rwkv_mu, w_decay0, w_decay_up, w_iclr0, w_iclr_up, k_k, k_a, r_k,
              ln_x_g, ln_x_b, w_out_rwkv, conv_w, conv_b, lru_gx_w, lru_gx_b, lru_ga_w, lru_ga_b,
              lru_lambda, w_out_lru, w_out, final_norm_g):
    Bp = x_prompt.shape[0]
    xp, xs = x_prompt, x_sample
    sp_sh, sp_wkv, sp_conv, sp_lru = [], [], [], []
    ss_sh, ss_wkv, ss_conv, ss_lru = [], [], [], []
    for l in range(DEPTH):
        weights = (norm_g[l], w_in[l], rwkv_mu[l], w_decay0[l], w_decay_up[l], w_iclr0[l], w_iclr_up[l],
                   k_k[l], k_a[l], r_k[l], ln_x_g[l], ln_x_b[l], w_out_rwkv[l], conv_w[l], conv_b[l],
                   lru_gx_w[l], lru_gx_b[l], lru_ga_w[l], lru_ga_b[l], lru_lambda[l], w_out_lru[l], w_out[l])
        xp, a1, a2, a3, a4 = mixer_layer(
            xp,
            jnp.zeros((Bp, SHIFT_W), xp.dtype),
            jnp.zeros((Bp, RWKV_HEADS, HEAD_SIZE, HEAD_SIZE), jnp.float32),
            jnp.zeros((Bp, CONV_W - 1, LRU_WIDTH), xp.dtype),
            jnp.zeros((Bp, LRU_WIDTH), jnp.float32),
            *weights)
        sp_sh.append(a1); sp_wkv.append(a2); sp_conv.append(a3); sp_lru.append(a4)
        xs, b1, b2, b3, b4 = mixer_layer(xs, state_shift[l], state_wkv[l], state_conv[l], state_lru[l], *weights)
        ss_sh.append(b1); ss_wkv.append(b2); ss_conv.append(b3); ss_lru.append(b4)
    y_prompt = rmsnorm(xp, final_norm_g)
    y_sample = rmsnorm(xs, final_norm_g)
    return (y_prompt, y_sample,
            jnp.stack(sp_sh), jnp.stack(sp_wkv), jnp.stack(sp_conv), jnp.stack(sp_lru),
            jnp.stack(ss_sh), jnp.stack(ss_wkv), jnp.stack(ss_conv), jnp.stack(ss_lru))
```

```python
import numpy as np
import concourse.bass as bass
import concourse.mybir as mybir
from concourse.bass_utils import run_bass_kernel_spmd

F32 = mybir.dt.float32
BF16 = mybir.dt.bfloat16
AF = mybir.ActivationFunctionType
ALU = mybir.AluOpType
AX = mybir.AxisListType

EPOCH = 6000
MAXOPS = None
MARKS = []
NCORES = 8
D = 2048
CW = 1024
INW = 10368
SHW = 3200
PT = 512
NS = 16
NCH = 8
G0, MU0, WD0, WI0, KK0, KA0, RK0, LG0, LB0, CW0, CB0, GXB0, GAB0, LAM0, NPV = (
    0, 16, 41, 49, 57, 65, 73, 81, 89, 97, 129, 137, 145, 153, 161)


class Prog:
    def __init__(self, nc):
        self.nc = nc
        self.ops = []
        self.res = {}
        self.eng = {"pe": nc.tensor, "act": nc.scalar, "dve": nc.vector,
                    "pool": nc.gpsimd, "sp": nc.sync}
        self.dma_count = {}
        self.nsem = 0
        self.fin = []
        self.eng_free = {e: 0.0 for e in self.eng}

    def add(self, eng, fn, reads=(), writes=(), dma=None):
        i = len(self.ops)
        if MAXOPS is not None and i >= MAXOPS:
            self.fin.append(0.0)
            return i
        deps = set()
        for t in reads:
            st = self.res.get(t)
            if st is not None and st[0] is not None:
                deps.add(st[0])
        for t in writes:
            st = self.res.get(t)
            if st is not None:
                if st[0] is not None:
                    deps.add(st[0])
                deps.update(st[1].values())
                deps.update(st[2])
        for t in reads:
            st = self.res.get(t)
            if st is None:
                st = [None, {}, []]
                self.res[t] = st
            if dma is not None:
                st[2].append(i)
            else:
                st[1][eng] = i
        for t in writes:
            self.res[t] = [i, {}, []]
        deps.discard(i)
        dn = None
        if dma is not None:
            dn = self.dma_count.get(dma, 0) + 1
            self.dma_count[dma] = dn
        self.ops.append(dict(eng=eng, fn=fn, deps=deps, dma=dma, dn=dn))
        st_ = self.eng_free[eng]
        for d_ in deps:
            if self.fin[d_] + 0.2 > st_:
                st_ = self.fin[d_] + 0.2
        du_ = _est(eng, fn, dma) if fn is not None else 0.0
        if dma is not None:
            self.eng_free[eng] = st_ + (1.2 if eng == "pool" else 0.1)
        else:
            self.eng_free[eng] = st_ + du_
        self.fin.append(st_ + du_)
        return i

    def _sem(self, name):
        self.nsem += 1
        return self.nc.alloc_semaphore(name=name)

    def emit(self):
        ops = self.ops
        seen = {e: {} for e in self.eng}
        snaps = [None] * len(ops)
        waits = [None] * len(ops)
        need_sig = [False] * len(ops)
        for i, op in enumerate(ops):
            E = op["eng"]
            s = seen[E]
            wl = []
            dl = sorted(op["deps"])
            mx = {}
            for d in dl:
                if ops[d]["dma"] is not None:
                    mx[ops[d]["dma"]] = d
            for d in dl:
                p = ops[d]
                if p["dma"] is not None and mx[p["dma"]] != d:
                    continue
                if p["dma"] is not None:
                    k = ("dma", p["dma"])
                    if s.get(k, 0) >= p["dn"]:
                        continue
                    wl.append(d)
                    for kk, vv in snaps[d].items():
                        if s.get(kk, -1) < vv:
                            s[kk] = vv
                    s[k] = p["dn"]
                else:
                    Fe = p["eng"]
                    if Fe == "pe" and E == "pe":
                        continue
                    if s.get(Fe, -1) >= d:
                        continue
                    wl.append(d)
                    need_sig[d] = True
                    for kk, vv in snaps[d].items():
                        if s.get(kk, -1) < vv:
                            s[kk] = vv
                    s[Fe] = d
            waits[i] = wl
            snaps[i] = dict(s)
        sig = {}
        ecount = {e: 0 for e in self.eng}
        esem = {e: [] for e in self.eng}
        dsem = {}
        nwait = 0
        for i, op in enumerate(ops):
            E = op["eng"]
            h = self.eng[E]
            for d in waits[i]:
                sm, val = sig[d]
                h.wait_ge(sm, val)
                nwait += 1
            inst = op["fn"](h) if op["fn"] is not None else None
            if op["dma"] is not None:
                key = op["dma"]
                if key not in dsem:
                    dsem[key] = self._sem("d_" + str(key))
                inst.then_inc(dsem[key], 16)
                sig[i] = (dsem[key], 16 * op["dn"])
            elif need_sig[i]:
                c = ecount[E]
                ep = c // EPOCH
                while len(esem[E]) <= ep:
                    esem[E].append(self._sem("e_%s_%d" % (E, len(esem[E]))))
                inst.then_inc(esem[E][ep], 1)
                ecount[E] = c + 1
                sig[i] = (esem[E][ep], c % EPOCH + 1)
        self.stats = dict(nops=len(ops), nwait=nwait, nsem=self.nsem, sigs=dict(ecount))
        return self.stats


def I(method, *args, **kw):
    f = lambda e: getattr(e, method)(*args, **kw)
    f.meta = (method, args, kw)
    return f


def _est(eng, fn, dma):
    n = 256
    try:
        method, args, kw = fn.meta
        o = kw.get("out", args[0] if args else None)
        n = 1
        for d_ in o.shape[1:]:
            n *= int(d_)
    except Exception:
        method = ""
    if dma is not None:
        return 5.0
    if eng == "pe":
        return 0.07 + n / 1400.0
    if eng == "act":
        return 0.2 + n / 1200.0
    if eng == "dve":
        return (0.15 + n / 480.0) if method == "tensor_tensor_scan" else (0.15 + n / 960.0)
    return 0.5 + n / 500.0


class Rot:
    def __init__(self, items):
        self.items = list(items)
        self.i = 0

    def next(self):
        v = self.items[self.i % len(self.items)]
        self.i += 1
        return v


def build(passes):
    nc = bass.Bass("TRN2", target_bir_lowering=False)
    P = Prog(nc)
    add = P.add

    def din(name, shape):
        return nc.dram_tensor(name, shape, F32, kind="ExternalInput").ap()

    def dout(name, shape):
        return nc.dram_tensor(name, shape, F32, kind="ExternalOutput").ap()

    xsrc = {"xprev": din("xprev", [1024, D]), "xown": din("xown", [1024, D])}
    xs_d = din("xs", [NS, D])
    flag_d = din("flag", [128, 1])
    sshift_d = din("sshift", [NS, SHW])
    swkv_d = din("swkv", [NS, 16, 64, 64])
    sconv_d = din("sconv", [NS, 3, 1024])
    slru_d = din("slru", [NS, 1024])
    pvec_d = din("pvec", [128, NPV])
    w_in = din("w_in", [INW // 128, 128, 16, 128])
    wdu_d = din("wdu", [64, CW])
    wiu_d = din("wiu", [64, CW])
    worw_d = din("worw", [16, 128, 8, 128])
    wolru_d = din("wolru", [16, 128, 8, 128])
    gxw_d = din("gxw", [16, 64, 64])
    gaw_d = din("gaw", [16, 64, 64])
    wout_d = din("wout", [4, 4, 128, 4, 512])
    fng_d = din("fng", [1, D])
    yp_d = dout("yp", [1024, D])
    ys_d = dout("ys", [NS, D])
    o_shp = dout("o_shp", [25, 128])
    o_wkvp = dout("o_wkvp", [16, 64, 64])
    o_convp = dout("o_convp", [3, 8, 128])
    o_lrup = dout("o_lrup", [8, 128])
    o_shs = dout("o_shs", [NS, SHW])
    o_wkvs = dout("o_wkvs", [NS, 16, 64, 64])
    o_convs = dout("o_convs", [NS, 3, 1024])
    o_lrus = dout("o_lrus", [NS, 1024])
    out_tokens = []

    def sb(name, shape, dt):
        return nc.sbuf_tensor("s_" + name, shape, dt).__enter__()

    pb = [nc.psum_tensor("pb%d" % i, [128, 512], F32).__enter__() for i in range(8)]
    zpair = Rot([(0, 1), (2, 3)])
    pairr = Rot([((5, 7)[i % 2], 6, (i % 32) * NS) for i in range(32)])
    nbr = Rot([5, 6, 7])
    nbr2 = Rot([4, 5, 6, 7])

    def pbt(b):
        return "pb%d" % b

    ident_f = sb("ident_f", [128, 128], F32)
    ident_b = sb("ident_b", [128, 128], BF16)
    onesblk = sb("onesblk", [128, 128], F32)
    E2 = sb("E2", [128, 64], F32)
    ones_b = sb("ones_b", [128, 128], BF16)
    I16 = sb("I16", [64, 64], BF16)
    mask4 = sb("mask4", [64, 4, 128], BF16)
    maskY = sb("maskY", [64, 8, 64], BF16)
    segmask = sb("segmask", [128, 512], BF16)
    pvec = sb("pvec", [128, NPV], F32)
    omka = sb("omka", [128, 8], F32)
    nsp = sb("nsp", [128, 8], F32)
    nsp2 = sb("nsp2", [128, 8], F32)
    flag = sb("flag", [128, 1], F32)
    ones_c = sb("ones_c", [128, 1], F32)
    lora_w = sb("lora_w", [128, CW], BF16)
    GXW = sb("GXW", [128, 8, 128], BF16)
    GAW = sb("GAW", [128, 8, 128], BF16)
    shiftT = sb("shiftT", [128, 25, NS], F32)
    convT = sb("convT", [128, 8, 3, NS], F32)
    hlruT = sb("hlruT", [128, 8, NS], F32)
    shiftOutT = sb("shiftOutT", [128, 25, NS], F32)
    LRUS = sb("LRUS", [128, 8, NS], F32)
    CONVS = sb("CONVS", [128, 8, NS], F32)
    NCOLMAX = PT + NS
    AR1 = sb("ARENA1", [128, 16 * NCOLMAX], BF16)
    xn = AR1[:, 0:4 * D].rearrange("p (t d) -> p t d", t=4)
    MT = AR1[:, 0:16 * NCOLMAX].rearrange("p (j c) -> p j c", j=16)
    hT = sb("hT", [128, 16, NCOLMAX], BF16)
    WB = [sb("WB%d" % i, [128, 16, 128], BF16) for i in range(4)]
    wrot = Rot(range(4))
    NW = 22
    WA = sb("WA", [128, NW * NCOLMAX], F32)
    W = [WA[:, i * NCOLMAX:(i + 1) * NCOLMAX] for i in range(NW)]
    wt = ["W%d" % i for i in range(NW)]

    def wtoks(c0, c1):
        return [wt[i] for i in range(c0 // NCOLMAX, (c1 - 1) // NCOLMAX + 1)]
    XT = [WA[:, 0:D], WA[:, D:2 * D]]
    xtt = [wtoks(0, D), wtoks(D, 2 * D)]
    LWD = sb("LWD", [128, NCOLMAX], BF16)
    LWI = sb("LWI", [128, NCOLMAX], BF16)
    ARzP = [[sb("ARz%d_%d" % (p_, i), [128, NCH, 128], BF16) for i in range(2)] for p_ in range(2)]
    KBz = [sb("KBz%d" % i, [128, NCH, 128], BF16) for i in range(2)]
    KB3 = sb("KB3", [128, NCH, 128], BF16)
    VB = sb("VB", [128, PT], BF16)
    TOKP = [sb("TOK_%d" % p_, [64, 3, NCH, 128], BF16) for p_ in range(2)]
    MSKP = [sb("MSK_%d" % p_, [64, 16, 128], BF16) for p_ in range(2)]
    MSBP = [sb("MSB_%d" % p_, [64, 16, 128], BF16) for p_ in range(2)]
    TTbP = [sb("TTb_%d" % p_, [64, 16, 64], BF16) for p_ in range(2)]
    BONb = [sb("BONb%d" % i, [128, NCOLMAX], BF16) for i in range(4)]
    XA = sb("XA", [64, 16, 64], BF16)
    YA = sb("YA", [64, 16, 64], BF16)
    XBt = sb("XBt", [64, 16, 64], BF16)
    YBt = sb("YBt", [64, 16, 64], BF16)
    Y1 = YBt
    QA = sb("QA", [64, 16, 64], BF16)
    QB = sb("QB", [64, 16, 64], BF16)
    WS = sb("WS", [64, 128], BF16)
    US = sb("US", [64, NCH, 128], BF16)
    Hball = sb("Hball", [128, NCH + 1, 64], BF16)
    XCB0 = sb("XCB", [128, NCOLMAX], BF16)
    Hf = sb("Hf", [128, 8, 64], F32)
    HDL = sb("HDL", [128, 64], F32)
    SWt = sb("SWt", [128, NS, 64], F32)
    shc = sb("shc", [128, 25], F32)
    convc = sb("convc", [128, 3, 8], F32)
    hcar = sb("hcar", [128, 8], F32)
    ssq = sb("ssq", [128, 8], F32)
    tq1 = sb("tq1", [128, 8], F32)
    tq2 = sb("tq2", [128, 8], F32)
    rstd = sb("rstd", [128, 8], F32)
    DLtP = [sb("DLt%d" % p_, [128, NCH], F32) for p_ in range(2)]
    SMP2 = sb("SMP", [128, 3, 6, NS], F32)
    omu = sb("omu", [128, 25], F32)
    nkk = sb("nkk", [128, 8], F32)
    SA = sb("SA", [128, NS], F32)
    ORT = sb("ORT", [128, 8, NCOLMAX], BF16)
    xns = ORT[:].rearrange("p a b -> p (a b)")[:, 0:D]
    OGT = sb("OGT", [128, 8, NCOLMAX], BF16)
    WKVO = WA[0:64, 0:1024].rearrange("p (a b) -> p a b", b=128)
    SMALLO = WA[0:32, 0:128]

    def pv(off, i):
        return pvec[:, off + i:off + i + 1]

    add("sp", I("dma_start", out=pvec[:], in_=pvec_d[:, :]), writes=["pvec"], dma="pvec")
    add("sp", I("dma_start", out=flag[:], in_=flag_d[:, :]), writes=["flag"], dma="flag")
    add("pool", I("dma_start", out=lora_w[0:64, :], in_=wdu_d[:, :]), writes=["lora_w0"], dma="lw0")
    add("pool", I("dma_start", out=lora_w[64:128, :], in_=wiu_d[:, :]), writes=["lora_w1"], dma="lw1")
    add("dve", I("memset", GXW[:], 0.0), writes=["GXW"])
    add("dve", I("memset", GAW[:], 0.0), writes=["GAW"])
    for h2 in range(2):
        src = gxw_d.rearrange("(hb h2) i j -> h2 i hb j", h2=2)[h2]
        add("pool", I("dma_start", out=GXW[h2 * 64:(h2 + 1) * 64, :, h2 * 64:(h2 + 1) * 64], in_=src),
            reads=[], writes=["GXW"], dma="gxw")
        src = gaw_d.rearrange("(hb h2) i j -> h2 i hb j", h2=2)[h2]
        add("pool", I("dma_start", out=GAW[h2 * 64:(h2 + 1) * 64, :, h2 * 64:(h2 + 1) * 64], in_=src),
            reads=[], writes=["GAW"], dma="gaw")
    add("pool", I("memset", ident_f[:], 0.0), writes=["ident_f"])
    add("pool", I("affine_select", out=ident_f[:], in_=ident_f[:], pattern=[[-1, 128]],
                  compare_op=ALU.not_equal, fill=1.0, base=0, channel_multiplier=1),
        reads=["ident_f"], writes=["ident_f"])
    add("dve", I("tensor_copy", out=ident_b[:], in_=ident_f[:]), reads=["ident_f"], writes=["ident_b"])
    add("dve", I("memset", onesblk[:], 0.0), writes=["onesblk"])
    add("dve", I("memset", onesblk[0:64, 0:64], 1.0), reads=["onesblk"], writes=["onesblk"])
    add("dve", I("memset", onesblk[64:128, 64:128], 1.0), reads=["onesblk"], writes=["onesblk"])
    add("dve", I("tensor_copy", out=ones_b[:], in_=onesblk[:]), reads=["onesblk"], writes=["ones_b"])
    add("dve", I("tensor_tensor", out=E2[:], in0=ident_f[:, 0:64], in1=ident_f[:, 64:128], op=ALU.add),
        reads=["ident_f"], writes=["E2"])
    add("dve", I("tensor_copy", out=I16[:], in_=ident_f[0:64, 0:64]), reads=["ident_f"], writes=["I16"])
    add("dve", I("memset", LWD[:], 0.0), writes=["LWt"])
    add("dve", I("memset", LWI[:], 0.0), writes=["LWt"])
    add("pool", I("memset", mask4[:], 1.0), writes=["mask4"])
    for q in range(4):
        add("pool", I("affine_select", out=mask4[:, q, 0:64], in_=mask4[:, q, 0:64], pattern=[[1, 64]],
                      compare_op=ALU.is_gt, fill=0.0, base=0, channel_multiplier=-1),
            reads=["mask4"], writes=["mask4"])
        add("pool", I("affine_select", out=mask4[:, q, 64:128], in_=mask4[:, q, 64:128], pattern=[[1, 64]],
                      compare_op=ALU.is_ge, fill=0.0, base=0, channel_multiplier=-1),
            reads=["mask4"], writes=["mask4"])
    add("pool", I("memset", maskY[:], 1.0), writes=["maskY"])
    for q in range(8):
        add("pool", I("affine_select", out=maskY[:, q, :], in_=maskY[:, q, :], pattern=[[-1, 64]],
                      compare_op=ALU.is_gt, fill=0.0, base=0, channel_multiplier=1),
            reads=["maskY"], writes=["maskY"])
    add("dve", I("memset", ones_c[:], 1.0), writes=["ones_c"])
    add("dve", I("memset", segmask[:], 1.0), writes=["segmask"])
    add("dve", I("memset", segmask[:].rearrange("p (c l) -> p c l", l=64)[:, :, 0:1], 0.0),
        reads=["segmask"], writes=["segmask"])
    add("dve", I("tensor_scalar", out=omka[:], in0=pvec[:, KA0:KA0 + 8], scalar1=-1.0, scalar2=1.0,
                 op0=ALU.mult, op1=ALU.add), reads=["pvec"], writes=["omka"])
    add("dve", I("tensor_scalar", out=omu[:], in0=pvec[:, MU0:MU0 + 25], scalar1=-1.0, scalar2=1.0,
                 op0=ALU.mult, op1=ALU.add), reads=["pvec"], writes=["omu"])
    add("dve", I("tensor_scalar", out=nkk[:], in0=pvec[:, KK0:KK0 + 8], scalar1=-1.0, scalar2=None,
                 op0=ALU.mult), reads=["pvec"], writes=["nkk"])
    add("act", I("activation", out=tq1[:], in_=pvec[:, LAM0:LAM0 + 8], func=AF.Exp, scale=-1.0),
        reads=["pvec"], writes=["tq1"])
    add("dve", I("tensor_scalar", out=tq2[:], in0=tq1[:], scalar1=1.0, scalar2=None, op0=ALU.add),
        reads=["tq1"], writes=["tq2"])
    add("act", I("activation", out=tq1[:], in_=tq2[:], func=AF.Ln), reads=["tq2"], writes=["tq1"])
    add("dve", I("tensor_scalar", out=nsp[:], in0=tq1[:], scalar1=-8.0, scalar2=None, op0=ALU.mult),
        reads=["tq1"], writes=["nsp"])
    add("dve", I("tensor_scalar", out=nsp2[:], in0=tq1[:], scalar1=-16.0, scalar2=None, op0=ALU.mult),
        reads=["tq1"], writes=["nsp2"])
    for i_ in range(2):
        for p_ in range(2):
            add("dve", I("memset", ARzP[p_][i_][:], 0.0), writes=["AR3p%d" % p_])
        add("dve", I("memset", KBz[i_][:], 0.0), writes=["KBz"])
    add("dve", I("memset", Hf[:], 0.0), writes=["Hf"])
    add("dve", I("memset", shc[:], 0.0), writes=["shc"])
    add("dve", I("memset", convc[:], 0.0), writes=["convc"])
    add("dve", I("memset", hcar[:], 0.0), writes=["hcar"])

    stg = WA[0:NS, 0:SHW]
    stg_tok = wt[0:7]

    def load_T(src_ap, ncols, dst_fn, dst_tok):
        add("sp", I("dma_start", out=stg[:, 0:ncols], in_=src_ap), writes=stg_tok, dma="stg")
        nblk = ncols // 128
        for g0 in range(0, nblk, 16):
            b = nbr.next()
            nb_ = min(16, nblk - g0)
            for q in range(nb_):
                blk = g0 + q
                add("pe", I("transpose", pb[b][:, q * NS:(q + 1) * NS], stg[:, blk * 128:(blk + 1) * 128],
                            ident_f[0:NS, 0:NS]), reads=stg_tok + ["ident_f"], writes=[pbt(b)])
            dst_fn(b, g0, nb_)

    def sh_dst(b, g0, n):
        add("dve", I("tensor_copy", out=shiftT[:, g0:g0 + n, :],
                     in_=pb[b][:, 0:n * NS].rearrange("p (a s) -> p a s", s=NS)),
            reads=[pbt(b)], writes=["shiftT"])

    load_T(sshift_d[:, :], SHW, sh_dst, "shiftT")

    def conv_dst(b, g0, n):
        for q in range(n):
            blk = g0 + q
            j, lb = blk // 8, blk % 8
            add("dve", I("tensor_copy", out=convT[:, lb, j, :], in_=pb[b][:, q * NS:(q + 1) * NS]),
                reads=[pbt(b)], writes=["convT"])

    load_T(sconv_d.rearrange("s j c -> s (j c)"), 3072, conv_dst, "convT")

    def lru_dst(b, g0, n):
        add("dve", I("tensor_copy", out=hlruT[:, g0:g0 + n, :],
                     in_=pb[b][:, 0:n * NS].rearrange("p (a s) -> p a s", s=NS)),
            reads=[pbt(b)], writes=["hlruT"])

    load_T(slru_d[:, :], 1024, lru_dst, "hlruT")

    def cols(pair, samples):
        r = [(pair[0], 0, PT, 0)]
        if samples:
            r.append((pair[1], pair[2] if len(pair) > 2 else 0, NS, PT))
        return r

    def zblock(col0, samples, wsrc=None, kchunks=16):
        slot = wrot.next()
        src = (w_in if wsrc is None else wsrc)[col0 // 128]
        add("pool", I("dma_start", out=WB[slot][:, 0:kchunks, :], in_=src),
            writes=["WB%d" % slot], dma="WB%d" % slot)
        return slot

    def zmm(slot, pair, samples, rhs_tile, rhs_tok, kchunks=16):
        wtile, wtok = (WB[slot], "WB%d" % slot) if isinstance(slot, int) else slot
        for (b, pc, n, tc) in cols(pair, samples):
            for kc in range(kchunks):
                add("pe", I("matmul", pb[b][:, pc:pc + n], lhsT=wtile[:, kc, :],
                            rhs=rhs_tile[:, kc, tc:tc + n], start=(kc == 0), stop=(kc == kchunks - 1)),
                    reads=[wtok, rhs_tok], writes=[pbt(b)])

    def evac(eng, pair, samples, dst, dtok, method="copy", **kw):
        for (b, pc, n, tc) in cols(pair, samples):
            if eng == "act":
                add("act", I("activation", out=dst[:, tc:tc + n], in_=pb[b][:, pc:pc + n], **kw),
                    reads=[pbt(b)] + kw.pop("_r", []) if False else [pbt(b)], writes=[dtok])
            else:
                add("dve", I("tensor_copy", out=dst[:, tc:tc + n], in_=pb[b][:, pc:pc + n]),
                    reads=[pbt(b)], writes=[dtok])

    def stage_x(ps):
        src = xsrc[ps["src"]]
        row0 = ps["row0"]
        samples = ps["samples"]
        tiles = [(tt, 128) for tt in range(4)] + ([(4, NS)] if samples else [])
        for (tt, m) in tiles:
            bx = tt % 2
            if tt < 4:
                add("sp", I("dma_start", out=XT[bx][:, :], in_=src[row0 + tt * 128:row0 + (tt + 1) * 128, :]),
                    writes=xtt[bx], dma="XT%d" % bx)
                dst = xn[:, tt, :]
            else:
                add("sp", I("dma_start", out=XT[bx][0:NS, :], in_=xs_d[:, :]),
                    writes=xtt[bx], dma="XT%d" % bx)
                dst = xns
            add("act", I("activation", out=dst[0:m, :], in_=XT[bx][0:m, :], func=AF.Square,
                         accum_out=ssq[0:m, tt:tt + 1]),
                reads=xtt[bx], writes=["ARENA1", "ORT", "ssq"])
            add("dve", I("tensor_scalar", out=tq1[0:m, tt:tt + 1], in0=ssq[0:m, tt:tt + 1],
                         scalar1=1.0 / D, scalar2=1e-6, op0=ALU.mult, op1=ALU.add),
                reads=["ssq"], writes=["tq1"])
            add("act", I("activation", out=tq2[0:m, tt:tt + 1], in_=tq1[0:m, tt:tt + 1], func=AF.Ln),
                reads=["tq1"], writes=["tq2"])
            add("act", I("activation", out=rstd[0:m, tt:tt + 1], in_=tq2[0:m, tt:tt + 1], func=AF.Exp, scale=-0.5),
                reads=["tq2"], writes=["rstd"])
            add("act", I("activation", out=dst[0:m, :], in_=XT[bx][0:m, :], func=AF.Identity,
                         scale=rstd[0:m, tt:tt + 1]),
                reads=xtt[bx] + ["rstd"], writes=["ARENA1", "ORT"])
        ncol = PT + (NS if samples else 0)
        for k in range(16):
            b = nbr.next()
            pv_ = pb[b][:].bitcast(BF16)
            for (tt, m) in tiles:
                if tt < 4:
                    add("pe", I("transpose", pv_[:, tt * 128:(tt + 1) * 128], xn[:, tt, k * 128:(k + 1) * 128],
                                ident_b[:]), reads=["ARENA1", "ORT", "ident_b"], writes=[pbt(b)])
                else:
                    add("pe", I("transpose", pv_[:, PT:PT + NS], xns[0:NS, k * 128:(k + 1) * 128],
                                ident_b[0:NS, 0:NS]), reads=["ARENA1", "ORT", "ident_b"], writes=[pbt(b)])
            add("dve", I("tensor_scalar", out=hT[:, k, 0:ncol], in0=pv_[:, 0:ncol], scalar1=pv(G0, k),
                         scalar2=None, op0=ALU.mult), reads=[pbt(b), "pvec"], writes=["hT"])

    def shift_block(blk, pair, ps, dst, dtok):
        samples = ps["samples"]
        ncol = PT + (NS if samples else 0)
        Zr = W[0]
        for (b, pc, n, tc) in cols(pair, samples):
            add("act", I("copy", out=Zr[:, tc:tc + n], in_=pb[b][:, pc:pc + n]), reads=[pbt(b)], writes=[wt[0]])
            add("act", I("activation", out=dst[:, tc:tc + n], in_=pb[b][:, pc:pc + n], func=AF.Identity,
                         scale=omu[:, blk:blk + 1]), reads=[pbt(b), "omu"], writes=[dtok])
        yield
        add("dve", I("scalar_tensor_tensor", out=dst[:, 1:PT], in0=Zr[:, 0:PT - 1], scalar=pv(MU0, blk),
                     in1=dst[:, 1:PT], op0=ALU.mult, op1=ALU.add), reads=[wt[0], dtok, "pvec"], writes=[dtok])
        add("dve", I("scalar_tensor_tensor", out=dst[:, 0:1], in0=shc[:, blk:blk + 1], scalar=pv(MU0, blk),
                     in1=dst[:, 0:1], op0=ALU.mult, op1=ALU.add), reads=["shc", dtok, "pvec"], writes=[dtok])
        if samples:
            add("dve", I("scalar_tensor_tensor", out=dst[:, PT:PT + NS], in0=shiftT[:, blk, :], scalar=pv(MU0, blk),
                         in1=dst[:, PT:PT + NS], op0=ALU.mult, op1=ALU.add), reads=["shiftT", dtok, "pvec"],
                writes=[dtok])
            add("act", I("copy", out=shiftOutT[:, blk, :], in_=Zr[:, PT:PT + NS]), reads=[wt[0]],
                writes=["shiftOutT"])
        add("act", I("copy", out=shc[:, blk:blk + 1], in_=Zr[:, PT - 1:PT]), reads=[wt[0], "shc"], writes=["shc"])
        yield

    def ones_mm(src, stok, samples):
        pair = pairr.next()
        for (b, pc, n, tc) in cols(pair, samples):
            add("pe", I("matmul", pb[b][:, pc:pc + n], lhsT=onesblk[:], rhs=src[:, tc:tc + n], start=True,
                        stop=True), reads=["onesblk", stok], writes=[pbt(b)])
        return pair

    def zmm_g(slot, pair, samples, rhs_tile, rhs_tok, kchunks=16):
        for (b, pc, n, tc) in cols(pair, samples):
            for kc in range(kchunks):
                add("pe", I("matmul", pb[b][:, pc:pc + n], lhsT=WB[slot][:, kc, :],
                            rhs=rhs_tile[:, kc, tc:tc + n], start=(kc == 0), stop=(kc == kchunks - 1)),
                    reads=["WB%d" % slot, rhs_tok], writes=[pbt(b)])
                if kc % 4 == 3:
                    yield

    Rt, Kt, Vt, LOGD, AI, AV, KM, BV, CL, EP, EM = (W[3], W[4], W[5], W[6], W[7], W[9], W[10], W[11], W[15],
                                                    W[16], W[17])
    tR, tK, tV, tLOGD, tAI, tAV, tKM, tBV, tCL, tEP, tEM = (wt[3], wt[4], wt[5], wt[6], wt[7], wt[9], wt[10],
                                                            wt[11], wt[15], wt[16], wt[17])
    T1, T2, tT1, tT2 = W[2], W[12], wt[2], wt[12]
    BONs, tBONs = BONb, ["BON%d" % i for i in range(4)]
    YT, T1p, T2p, SILp = W[19], W[8], W[18], W[1]
    tYT, tT1p, tT2p, tSILp = wt[19], wt[8], wt[18], wt[1]
    DSC = 0.6065306597126334
    rows = [slice(0, 64), slice(64, 128)]

    def part1(hp, ps):
        samples, full = ps["samples"], ps["full"]
        ncol = PT + (NS if samples else 0)
        BON, tBON = BONs[hp % 4], tBONs[hp % 4]
        rmode = "full" if full else ps.get("rmode", "full")
        slots = [zblock(hp * 128, samples) if rmode != "none" else None,
                 zblock(1024 + hp * 128, samples), zblock(2048 + hp * 128, samples)]
        yield
        for gi, (dst, dtok) in enumerate([(Rt, tR), (Kt, tK), (Vt, tV)]):
            if gi == 0 and rmode == "none":
                continue
            pair = zpair.next()
            if gi == 0 and rmode == "last":
                for kc in range(16):
                    add("pe", I("matmul", pb[pair[0]][:, 0:1], lhsT=WB[slots[0]][:, kc, :],
                                rhs=hT[:, kc, PT - 1:PT], start=(kc == 0), stop=(kc == 15)),
                        reads=["WB%d" % slots[0], "hT"], writes=[pbt(pair[0])])
                add("act", I("copy", out=shc[:, hp:hp + 1], in_=pb[pair[0]][:, 0:1]), reads=[pbt(pair[0]), "shc"],
                    writes=["shc"])
                yield
                continue
            yield from zmm_g(slots[gi], pair, samples, hT, "hT")
            yield from shift_block(gi * 8 + hp, pair, ps, dst, dtok)
        pD = pairr.next()
        pI = pairr.next()
        for (pr, lw_) in [(pD, LWD), (pI, LWI)]:
            for (b, pc, n, tc) in cols(pr, samples):
                add("pe", I("matmul", pb[b][:, pc:pc + n], lhsT=lora_w[:, hp * 128:(hp + 1) * 128],
                            rhs=lw_[:, tc:tc + n], start=True, stop=True),
                    reads=["lora_w0", "lora_w1", "LWt"], writes=[pbt(b)])
        for (b, pc, n, tc) in cols(pD, samples):
            add("act", I("activation", out=LOGD[:, tc:tc + n], in_=pb[b][:, pc:pc + n], func=AF.Sigmoid,
                         bias=pv(WD0, hp)), reads=[pbt(b), "pvec"], writes=[tLOGD])
        for (b, pc, n, tc) in cols(pI, samples):
            add("act", I("activation", out=AI[:, tc:tc + n], in_=pb[b][:, pc:pc + n], func=AF.Sigmoid,
                         bias=pv(WI0, hp)), reads=[pbt(b), "pvec"], writes=[tAI])
        yield
        add("act", I("activation", out=T1[:, 0:ncol], in_=Kt[:, 0:ncol], func=AF.Square, scale=pv(KK0, hp)),
            reads=[tK, "pvec"], writes=[tT1])
        yield
        pr = ones_mm(T1, tT1, samples)
        for (b, pc, n, tc) in cols(pr, samples):
            add("dve", I("tensor_scalar", out=T2[:, tc:tc + n], in0=pb[b][:, pc:pc + n], scalar1=1e-24,
                         scalar2=None, op0=ALU.max), reads=[pbt(b)], writes=[tT2])
        yield
        add("act", I("activation", out=T2[:, 0:ncol], in_=T2[:, 0:ncol], func=AF.Ln), reads=[tT2], writes=[tT2])
        add("act", I("activation", out=T2[:, 0:ncol], in_=T2[:, 0:ncol], func=AF.Exp, scale=-0.5), reads=[tT2],
            writes=[tT2])
        yield
        add("dve", I("scalar_tensor_tensor", out=AV[:, 0:ncol], in0=Kt[:, 0:ncol], scalar=nkk[:, hp:hp + 1],
                     in1=T2[:, 0:ncol], op0=ALU.mult, op1=ALU.mult), reads=[tK, tT2, "nkk"], writes=[tAV])
        yield
        add("dve", I("scalar_tensor_tensor", out=BV[:, 0:ncol], in0=AV[:, 0:ncol], scalar=-1.0,
                     in1=AI[:, 0:ncol], op0=ALU.mult, op1=ALU.mult), reads=[tAV, tAI], writes=[tBV])
        add("act", I("activation", out=T1[:, 0:ncol], in_=AI[:, 0:ncol], func=AF.Identity, scale=pv(KA0, hp),
                     bias=omka[:, hp:hp + 1]), reads=[tAI, "pvec", "omka"], writes=[tT1])
        yield
        add("dve", I("tensor_tensor", out=KM[:, 0:ncol], in0=Kt[:, 0:ncol], in1=T1[:, 0:ncol], op=ALU.mult),
            reads=[tK, tT1], writes=[tKM])
        yield
        if full:
            add("dve", I("scalar_tensor_tensor", out=T1[:, 0:ncol], in0=Rt[:, 0:ncol], scalar=pv(RK0, hp),
                         in1=KM[:, 0:ncol], op0=ALU.mult, op1=ALU.mult), reads=[tR, tKM, "pvec"], writes=[tT1])
            yield
            pr = ones_mm(T1, tT1, samples)
            for (b, pc, n, tc) in cols(pr, samples):
                add("dve", I("tensor_tensor", out=BON[:, tc:tc + n], in0=pb[b][:, pc:pc + n],
                             in1=Vt[:, tc:tc + n], op=ALU.mult), reads=[pbt(b), tV], writes=[tBON])
            yield
        add("dve", I("tensor_tensor_scan", out=CL[:, 0:PT], data0=segmask[:], data1=LOGD[:, 0:PT], initial=0.0,
                     op0=ALU.mult, op1=ALU.add), reads=["segmask", tLOGD], writes=[tCL])
        yield
        add("act", I("activation", out=EP[:, 0:PT], in_=CL[:, 0:PT], func=AF.Exp, scale=-DSC), reads=[tCL],
            writes=[tEP])
        add("act", I("activation", out=EM[:, 0:PT], in_=CL[:, 0:PT], func=AF.Exp, scale=DSC), reads=[tCL],
            writes=[tEM])
        yield
        add("dve", I("tensor_tensor", out=CL[:, 0:PT], in0=CL[:, 0:PT], in1=LOGD[:, 0:PT], op=ALU.subtract),
            reads=[tCL, tLOGD], writes=[tCL])
        add("act", I("activation", out=CL[:, 0:PT], in_=CL[:, 0:PT], func=AF.Exp, scale=-DSC), reads=[tCL],
            writes=[tCL])
        yield

    def c3(t):
        return t[:, 0:PT].rearrange("p (c l) -> p c l", l=64)

    def part2_build(hp, ps):
        samples, full = ps["samples"], ps["full"]
        par = hp % 2
        ARz, tAR = ARzP[par], "AR3p%d" % par
        DLt, tDL = DLtP[par], "DLt%d" % par
        EX, tEX = CL, tCL
        add("dve", I("tensor_tensor", out=KB3[:, :, 0:64], in0=c3(KM), in1=c3(EM), op=ALU.mult),
            reads=[tKM, tEM], writes=["KB3"])
        add("dve", I("tensor_tensor", out=KB3[:, :, 64:128], in0=c3(BV), in1=c3(EM), op=ALU.mult),
            reads=[tBV, tEM, "KB3"], writes=["KB3"])
        add("act", I("copy", out=VB[:], in_=Vt[:, 0:PT]), reads=[tV], writes=["VB"])
        for h2_ in range(2):
            r_ = rows[h2_]
            add("act", I("copy", out=KBz[h2_][r_, :, :], in_=KB3[r_, :, :]), reads=["KB3", "KBz"], writes=["KBz"])
        for h2_ in range(2):
            r_ = rows[h2_]
            add("dve", I("tensor_tensor", out=ARz[h2_][r_, :, 0:64], in0=c3(AV)[r_], in1=c3(EX)[r_], op=ALU.mult),
                reads=[tAV, tEX, tAR], writes=[tAR])
            if full:
                add("dve", I("tensor_tensor", out=ARz[h2_][r_, :, 64:128], in0=c3(Rt)[r_], in1=c3(EP)[r_],
                             op=ALU.mult), reads=[tR, tEP, tAR], writes=[tAR])
        add("act", I("copy", out=DLt[:], in_=EP[:, 0:PT].rearrange("p (c l) -> p c l", l=64)[:, :, 63]),
            reads=[tEP], writes=[tDL])
        if samples and full:
            SMP, tSMP = SMP2[:, hp % 3], "SMP%d" % (hp % 3)
            for i_, (t_, tk_) in enumerate([(AV, tAV), (BV, tBV), (KM, tKM), (Rt, tR), (Vt, tV)]):
                add("act", I("copy", out=SMP[:, i_, :], in_=t_[:, PT:PT + NS]), reads=[tk_, tSMP], writes=[tSMP])
            add("act", I("activation", out=SMP[:, 5, :], in_=LOGD[:, PT:PT + NS], func=AF.Exp, scale=-DSC),
                reads=[tLOGD, tSMP], writes=[tSMP])

    def part2(hp, ps):
        samples, full = ps["samples"], ps["full"]
        par = hp % 2
        ARz, tAR = ARzP[par], "AR3p%d" % par
        TOK, MSK, MSB = TOKP[par], MSKP[par], MSBP[par]
        tTOK = ["TOK%dp%d" % (i_, par) for i_ in range(3)]
        tMSK, tMSB = "MSKp%d" % par, "MSBp%d" % par
        nbq = nbr if full else nbr2
        for i3 in range(3):
            b = nbr.next()
            pv_ = pb[b][:].bitcast(BF16)
            for ch in range(NCH):
                if i3 == 0:
                    src = KB3[:, ch, 0:64]
                elif i3 == 1:
                    src = KB3[:, ch, 64:128]
                else:
                    src = VB[:, ch * 64:(ch + 1) * 64]
                add("pe", I("transpose", pv_[0:64, ch * 128:(ch + 1) * 128], src, ident_b[:]),
                    reads=["KB3" if i3 < 2 else "VB", "ident_b"], writes=[pbt(b)])
            add("act", I("copy", out=TOK[:, i3, :, :],
                         in_=pv_[0:64, 0:NCH * 128].rearrange("p (c m) -> p c m", m=128)),
                reads=[pbt(b)], writes=[tTOK[i3]])
            yield
        if full:
            for g in range(4):
                bK = nbr.next()
                bB = nbr.next()
                for q in range(4):
                    u = g * 4 + q
                    ch, h2 = u // 2, u % 2
                    add("pe", I("matmul", pb[bK][0:64, q * 128:(q + 1) * 128], lhsT=KBz[h2][:, ch, 0:64],
                                rhs=ARz[h2][:, ch, :], start=True, stop=True), reads=["KBz", tAR],
                        writes=[pbt(bK)])
                    add("pe", I("matmul", pb[bB][0:64, q * 128:(q + 1) * 128], lhsT=KBz[h2][:, ch, 64:128],
                                rhs=ARz[h2][:, ch, :], start=True, stop=True), reads=["KBz", tAR],
                        writes=[pbt(bB)])
                add("dve", I("tensor_tensor", out=MSK[:, g * 4:(g + 1) * 4, :],
                             in0=pb[bK][0:64, :].rearrange("p (a m) -> p a m", m=128), in1=mask4[:], op=ALU.mult),
                    reads=[pbt(bK), "mask4"], writes=[tMSK])
                add("dve", I("tensor_tensor", out=MSB[:, g * 4:(g + 1) * 4, :],
                             in0=pb[bB][0:64, :].rearrange("p (a m) -> p a m", m=128), in1=mask4[:], op=ALU.mult),
                    reads=[pbt(bB), "mask4"], writes=[tMSB])
                yield
        else:
            msk8 = mask4[:, 0:1, 0:64].broadcast_to([64, 8, 64])
            for g in range(2):
                bK = nbr.next()
                bB = nbr.next()
                for q in range(8):
                    u = g * 8 + q
                    ch, h2 = u // 2, u % 2
                    add("pe", I("matmul", pb[bK][0:64, q * 64:(q + 1) * 64], lhsT=KBz[h2][:, ch, 0:64],
                                rhs=ARz[h2][:, ch, 0:64], start=True, stop=True), reads=["KBz", tAR],
                        writes=[pbt(bK)])
                    add("pe", I("matmul", pb[bB][0:64, q * 64:(q + 1) * 64], lhsT=KBz[h2][:, ch, 64:128],
                                rhs=ARz[h2][:, ch, 0:64], start=True, stop=True), reads=["KBz", tAR],
                        writes=[pbt(bB)])
                add("dve", I("tensor_tensor", out=MSK[:, g * 8:(g + 1) * 8, 0:64],
                             in0=pb[bK][0:64, :].rearrange("p (a m) -> p a m", m=64), in1=msk8, op=ALU.mult),
                    reads=[pbt(bK), "mask4"], writes=[tMSK])
                add("dve", I("tensor_tensor", out=MSB[:, g * 8:(g + 1) * 8, 0:64],
                             in0=pb[bB][0:64, :].rearrange("p (a m) -> p a m", m=64), in1=msk8, op=ALU.mult),
                    reads=[pbt(bB), "mask4"], writes=[tMSB])
                yield
        for g in range(2):
            b = nbr.next()
            for q in range(8):
                u = g * 8 + q
                ch, h2 = u // 2, u % 2
                add("pe", I("matmul", pb[b][0:64, q * 64:(q + 1) * 64], lhsT=ARz[h2][:, ch, 0:64],
                            rhs=KBz[h2][:, ch, 64:128], start=True, stop=True), reads=["KBz", tAR],
                    writes=[pbt(b)])
            add("dve", I("tensor_tensor", out=Y1[:, g * 8:(g + 1) * 8, :],
                         in0=pb[b][0:64, :].rearrange("p (a m) -> p a m", m=64), in1=maskY[:], op=ALU.mult),
                reads=[pbt(b), "maskY"], writes=["YBtg%d" % g])
            yield
        add("dve", I("tensor_tensor", out=QA[:], in0=MSB[:, :, 0:64],
                     in1=I16[:].unsqueeze(1).broadcast_to([64, 16, 64]), op=ALU.add),
            reads=[tMSB, "I16"], writes=["QA"])

        def Xc(t, u):
            return MSB[:, u, 0:64] if t is None else t[:, u, :]
        curX, curXt, curY, curYt = None, [tMSB], Y1, ["YBtg0", "YBtg1"]
        nxt = [(XA, "XA", YA, "YA"), (XBt, "XBt", YBt, "YBt")]
        Q, Qt, Qn, Qnt = QA, "QA", QB, "QB"
        pend = None

        def q_update(nY_, nYt_, Q_, Qt_, Qn_, Qnt_):
            for g in range(2):
                bQ = nbq.next()
                for q in range(8):
                    u = g * 8 + q
                    add("pe", I("matmul", pb[bQ][0:64, q * 64:(q + 1) * 64], lhsT=nY_[:, u, :], rhs=Q_[:, u, :],
                                start=True, stop=True), reads=nYt_ + [Qt_], writes=[pbt(bQ)])
                add("dve", I("tensor_tensor", out=Qn_[:, g * 8:(g + 1) * 8, :],
                             in0=pb[bQ][0:64, :].rearrange("p (a m) -> p a m", m=64),
                             in1=Q_[:, g * 8:(g + 1) * 8, :], op=ALU.add), reads=[pbt(bQ), Qt_], writes=[Qnt_])
        for lev in range(5):
            nX, nXt, nY, nYt = nxt[lev % 2]
            last = lev == 4
            for g in range(2):
                bX = nbq.next() if not last else None
                bY = nbq.next()
                for q in range(8):
                    u = g * 8 + q
                    if not last:
                        add("pe", I("matmul", pb[bX][0:64, q * 64:(q + 1) * 64], lhsT=curY[:, u, :],
                                    rhs=Xc(curX, u), start=True, stop=True), reads=curXt + curYt,
                            writes=[pbt(bX)])
                    add("pe", I("matmul", pb[bY][0:64, q * 64:(q + 1) * 64], lhsT=Xc(curX, u),
                                rhs=curY[:, u, :], start=True, stop=True), reads=curXt + curYt, writes=[pbt(bY)])
                ev = "act" if g == 0 else "dve"
                cp = "copy" if g == 0 else "tensor_copy"
                add(ev, I(cp, out=nY[:, g * 8:(g + 1) * 8, :],
                          in_=pb[bY][0:64, :].rearrange("p (a m) -> p a m", m=64)),
                    reads=[pbt(bY)], writes=[nYt + "g%d" % g])
                if not last:
                    add(ev, I(cp, out=nX[:, g * 8:(g + 1) * 8, :],
                              in_=pb[bX][0:64, :].rearrange("p (a m) -> p a m", m=64)),
                        reads=[pbt(bX)], writes=[nXt + "g%d" % g])
            yield
            if pend is not None:
                q_update(*pend)
                yield
            pend = (nY, [nYt + "g0", nYt + "g1"], Q, Qt, Qn, Qnt)
            curX, curXt, curY, curYt = nX, [nXt + "g0", nXt + "g1"], nY, [nYt + "g0", nYt + "g1"]
            Q, Qt, Qn, Qnt = Qn, Qnt, Q, Qt
        q_update(*pend)
        yield
        add("act", I("copy", out=TTbP[par][:], in_=Q[:]), reads=[Qt], writes=["TTp%d" % par])
        yield

    BY = 4

    def chain(hp, ps):
        samples, full = ps["samples"], ps["full"]
        par = hp % 2
        ARz, tAR = ARzP[par], "AR3p%d" % par
        TOK, MSK, MSB = TOKP[par], MSKP[par], MSBP[par]
        tTOK = ["TOK%dp%d" % (i_, par) for i_ in range(3)]
        tMSK, tMSB = "MSKp%d" % par, "MSBp%d" % par
        TT, TTt = TTbP[par], "TTp%d" % par
        DLt, tDL = DLtP[par], "DLt%d" % par
        add("act", I("copy", out=Hball[:, 0, :], in_=Hf[:, hp, :]), reads=["Hf"], writes=["Hball"])
        for ch in range(NCH):
            bW = nbr.next()
            for h2 in range(2):
                u = ch * 2 + h2
                cs = slice(h2 * 64, (h2 + 1) * 64)
                add("pe", I("matmul", pb[bW][0:64, cs], lhsT=MSK[:, u, 0:64], rhs=TOK[:, 2, ch, cs], start=True,
                            stop=False), reads=[tMSK, tTOK[2]], writes=[pbt(bW)])
                add("pe", I("matmul", pb[bW][0:64, cs], lhsT=ARz[h2][:, ch, 0:64], rhs=Hball[:, ch, :],
                            start=False, stop=True), reads=[tAR, "Hball"], writes=[pbt(bW)])
            add("act", I("copy", out=WS[:], in_=pb[bW][0:64, 0:128]), reads=[pbt(bW)], writes=["WS"])
            yield
            bU = nbr.next()
            for h2 in range(2):
                u = ch * 2 + h2
                cs = slice(h2 * 64, (h2 + 1) * 64)
                add("pe", I("matmul", pb[bU][0:64, cs], lhsT=TT[:, u, :], rhs=WS[:, cs], start=True, stop=True),
                    reads=[TTt, "WS"], writes=[pbt(bU)])
            add("dve", I("tensor_copy", out=US[:, ch, :], in_=pb[bU][0:64, 0:128]), reads=[pbt(bU)],
                writes=["US"])
            yield
            bH = nbr.next()
            for h2 in range(2):
                rs_ = rows[h2]
                cs = slice(h2 * 64, (h2 + 1) * 64)
                add("pe", I("matmul", pb[bH][rs_, 0:64], lhsT=TOK[:, 0, ch, cs], rhs=TOK[:, 2, ch, cs],
                            start=True, stop=False), reads=[tTOK[0], tTOK[2]], writes=[pbt(bH)])
            for h2 in range(2):
                rs_ = rows[h2]
                cs = slice(h2 * 64, (h2 + 1) * 64)
                add("pe", I("matmul", pb[bH][rs_, 0:64], lhsT=TOK[:, 1, ch, cs], rhs=US[:, ch, cs],
                            start=False, stop=True), reads=[tTOK[1], "US"], writes=[pbt(bH)])
            dl = DLt[:, ch:ch + 1]
            add("act", I("activation", out=HDL[:], in_=Hf[:, hp, :], func=AF.Identity, scale=dl),
                reads=["Hf", tDL], writes=["HDL"])
            add("dve", I("scalar_tensor_tensor", out=Hball[:, ch + 1, :], in0=pb[bH][:, 0:64], scalar=dl, in1=HDL[:],
                         op0=ALU.mult, op1=ALU.add), reads=[pbt(bH), "HDL", tDL], writes=["Hball"])
            add("dve", I("scalar_tensor_tensor", out=Hf[:, hp, :], in0=pb[bH][:, 0:64], scalar=dl, in1=HDL[:],
                         op0=ALU.mult, op1=ALU.add), reads=[pbt(bH), "HDL", tDL], writes=["Hf"])
            yield
            if full:
                for h2 in range(2):
                    u = ch * 2 + h2
                    rs_ = rows[h2]
                    cs = slice(h2 * 64, (h2 + 1) * 64)
                    tcs = slice(ch * 64, (ch + 1) * 64)
                    add("pe", I("matmul", pb[BY][rs_, tcs], lhsT=Hball[:, ch, :], rhs=ARz[h2][:, ch, 64:128],
                                start=True, stop=False), reads=["Hball", tAR], writes=[pbt(BY)])
                    add("pe", I("matmul", pb[BY][rs_, tcs], lhsT=US[:, ch, cs], rhs=MSB[:, u, 64:128],
                                start=False, stop=False), reads=["US", tMSB], writes=[pbt(BY)])
                    add("pe", I("matmul", pb[BY][rs_, tcs], lhsT=TOK[:, 2, ch, cs], rhs=MSK[:, u, 64:128],
                                start=False, stop=True), reads=[tTOK[2], tMSK], writes=[pbt(BY)])
                yield

    def post(hp, ps):
        samples, full = ps["samples"], ps["full"]
        if not full:
            return
        ncol = PT + (NS if samples else 0)
        BON, tBON = BONs[hp % 4], tBONs[hp % 4]
        slot_rg = zblock(3200 + hp * 128, samples)
        yield
        if samples:
            SMP, tSMP = SMP2[:, hp % 3], "SMP%d" % (hp % 3)
            T3 = WA[:, 13 * NCOLMAX:13 * NCOLMAX + NS * 64]
            T3v = T3.rearrange("p (s k) -> p s k", k=64)
            DG = WA[:, 20 * NCOLMAX:20 * NCOLMAX + NS * 32].bitcast(BF16)
            DGv = DG.rearrange("p (s k) -> p s k", k=64)
            tT3, tDG = [wt[13], wt[14]], [wt[20], wt[21]]
            src = swkv_d.rearrange("s (hp h2) v k -> hp (h2 v) s k", h2=2)[hp]
            add("sp", I("dma_start", out=SWt[:], in_=src), writes=["SWt"], dma="SWt")

            def bcast(i_):
                add("dve", I("tensor_tensor", out=DGv, in0=E2[:].unsqueeze(1).broadcast_to([128, NS, 64]),
                             in1=SMP[:, i_, :].unsqueeze(2).broadcast_to([128, NS, 64]), op=ALU.mult),
                    reads=["E2", tSMP], writes=tDG)
                pr = (5, 7)
                for hf in range(2):
                    add("pe", I("matmul", pb[pr[hf]][:, 0:512], lhsT=ones_b[:], rhs=DG[:, hf * 512:(hf + 1) * 512],
                                start=True, stop=True), reads=["ones_b"] + tDG, writes=[pbt(pr[hf])])
                return pr

            def halves(pr):
                for hf in range(2):
                    yield (pb[pr[hf]][:, 0:512].rearrange("p (s k) -> p s k", k=64), pbt(pr[hf]),
                           slice(hf * 8, (hf + 1) * 8))
            pr = bcast(0)
            for (pv3, ptok, ss_) in halves(pr):
                add("dve", I("tensor_tensor", out=T3v[:, ss_, :], in0=SWt[:, ss_, :], in1=pv3, op=ALU.mult),
                    reads=["SWt", ptok], writes=tT3)
            add("dve", I("tensor_reduce", out=SA[:], in_=T3v, axis=AX.X, op=ALU.add), reads=tT3, writes=["SA"])
            yield
            pr = bcast(5)
            for (pv3, ptok, ss_) in halves(pr):
                add("dve", I("tensor_tensor", out=SWt[:, ss_, :], in0=SWt[:, ss_, :], in1=pv3, op=ALU.mult),
                    reads=["SWt", ptok], writes=["SWt"])
            yield
            pr = bcast(1)
            for (pv3, ptok, ss_) in halves(pr):
                add("dve", I("tensor_tensor", out=T3v[:, ss_, :], in0=pv3,
                             in1=SA[:, ss_].unsqueeze(2).broadcast_to([128, 8, 64]), op=ALU.mult),
                    reads=["SA", ptok], writes=tT3)
            add("dve", I("tensor_tensor", out=SWt[:], in0=SWt[:], in1=T3v, op=ALU.add), reads=["SWt"] + tT3,
                writes=["SWt"])
            yield
            pr = bcast(2)
            for (pv3, ptok, ss_) in halves(pr):
                add("dve", I("tensor_tensor", out=T3v[:, ss_, :], in0=pv3,
                             in1=SMP[:, 4, ss_].unsqueeze(2).broadcast_to([128, 8, 64]),
                             op=ALU.mult), reads=[tSMP, ptok], writes=tT3)
            add("dve", I("tensor_tensor", out=SWt[:], in0=SWt[:], in1=T3v, op=ALU.add), reads=["SWt"] + tT3,
                writes=["SWt"])
            yield
            pr = bcast(3)
            for (pv3, ptok, ss_) in halves(pr):
                add("dve", I("tensor_tensor", out=T3v[:, ss_, :], in0=SWt[:, ss_, :], in1=pv3, op=ALU.mult),
                    reads=["SWt", ptok], writes=tT3)
            add("dve", I("tensor_reduce", out=YT[:, PT:PT + NS], in_=T3v, axis=AX.X, op=ALU.add), reads=tT3,
                writes=[tYT])
            dst = o_wkvs.rearrange("s (hp h2) v k -> hp (h2 v) s k", h2=2)[hp]
            add("sp", I("dma_start", out=dst, in_=SWt[:]), reads=["SWt"], writes=["o_wkvs"], dma="o_wkvs")
            yield
        pr = ones_mm(YT, tYT, samples)
        for (b, pc, n, tc) in cols(pr, samples):
            add("act", I("activation", out=T2p[:, tc:tc + n], in_=pb[b][:, pc:pc + n], func=AF.Identity,
                         scale=1.0 / 64), reads=[pbt(b)], writes=[tT2p])
        add("act", I("activation", out=T1p[:, 0:ncol], in_=YT[:, 0:ncol], func=AF.Square), reads=[tYT],
            writes=[tT1p])
        yield
        add("dve", I("tensor_tensor", out=YT[:, 0:ncol], in0=YT[:, 0:ncol], in1=T2p[:, 0:ncol], op=ALU.subtract),
            reads=[tYT, tT2p], writes=[tYT])
        add("act", I("activation", out=T2p[:, 0:ncol], in_=T2p[:, 0:ncol], func=AF.Square), reads=[tT2p],
            writes=[tT2p])
        yield
        pr = ones_mm(T1p, tT1p, samples)
        for (b, pc, n, tc) in cols(pr, samples):
            add("dve", I("scalar_tensor_tensor", out=T1p[:, tc:tc + n], in0=pb[b][:, pc:pc + n], scalar=1.0 / 64,
                         in1=T2p[:, tc:tc + n], op0=ALU.mult, op1=ALU.subtract), reads=[pbt(b), tT2p],
                writes=[tT1p])
        add("dve", I("tensor_scalar", out=T1p[:, 0:ncol], in0=T1p[:, 0:ncol], scalar1=64e-5, scalar2=None,
                     op0=ALU.add), reads=[tT1p], writes=[tT1p])
        yield
        add("act", I("activation", out=T1p[:, 0:ncol], in_=T1p[:, 0:ncol], func=AF.Ln), reads=[tT1p],
            writes=[tT1p])
        add("act", I("activation", out=T1p[:, 0:ncol], in_=T1p[:, 0:ncol], func=AF.Exp, scale=-0.5), reads=[tT1p],
            writes=[tT1p])
        yield
        add("dve", I("tensor_tensor", out=YT[:, 0:ncol], in0=YT[:, 0:ncol], in1=T1p[:, 0:ncol], op=ALU.mult),
            reads=[tYT, tT1p], writes=[tYT])
        add("act", I("activation", out=YT[:, 0:ncol], in_=YT[:, 0:ncol], func=AF.Identity, scale=pv(LG0, hp),
                     bias=pv(LB0, hp)), reads=[tYT, "pvec"], writes=[tYT])
        yield
        add("dve", I("tensor_tensor", out=YT[:, 0:ncol], in0=YT[:, 0:ncol], in1=BON[:, 0:ncol], op=ALU.add),
            reads=[tYT, tBON], writes=[tYT])
        pair = pairr.next()
        for _ in zmm_g(slot_rg, pair, samples, hT, "hT"):
            pass
        for (b, pc, n, tc) in cols(pair, samples):
            add("act", I("activation", out=SILp[:, tc:tc + n], in_=pb[b][:, pc:pc + n], func=AF.Sigmoid),
                reads=[pbt(b)], writes=[tSILp])
            add("dve", I("tensor_tensor", out=SILp[:, tc:tc + n], in0=SILp[:, tc:tc + n], in1=pb[b][:, pc:pc + n],
                         op=ALU.mult), reads=[pbt(b), tSILp], writes=[tSILp])
        yield
        add("dve", I("tensor_tensor", out=ORT[:, hp, 0:ncol], in0=YT[:, 0:ncol], in1=SILp[:, 0:ncol], op=ALU.mult),
            reads=[tYT, tSILp], writes=["ORT"])
        yield

    def run_g(g):
        if g is None:
            return
        for _ in g:
            pass

    def interleave(*gr):
        alive = [[g, 0.0] for (g, r) in gr if g is not None]
        while alive:
            item = min(alive, key=lambda it: it[1])
            n0 = len(P.fin)
            try:
                next(item[0])
            except StopIteration:
                alive.remove(item)
                continue
            if len(P.fin) > n0:
                item[1] = max(P.fin[n0:])
            else:
                item[1] += 0.05

    def lru_block(lb, ps, ts_=0):
        samples, full = ps["samples"], ps["full"]
        ncol = PT + (NS if samples else 0)
        idx_ = [0, 1, 3, 4, 5, 6, 7, 8, 9, 10] if ts_ == 0 else [11, 12, 13, 14, 15, 16, 17, 18, 19, 20]
        XP, XC, GX, GA, A_, E2t, MU, SILg, HS, XS = [W[i_] for i_ in idx_]
        tXP, tXC, tGX, tGA, tA, tE2t, tMU, tSIL, tHS, tXS = [wt[i_] for i_ in idx_]
        XCB, tXCB = (XCB0, "XCB") if ts_ == 0 else (KB3[:].rearrange("p a b -> p (a b)")[:, 0:NCOLMAX], "KB3")
        slot = zblock(4224 + lb * 128, samples)
        slot2 = zblock(5248 + lb * 128, samples) if full else None
        pair = zpair.next()
        zmm(slot, pair, samples, hT, "hT")
        add("act", I("copy", out=XP[:, 3:3 + PT], in_=pb[pair[0]][:, 0:PT]), reads=[pbt(pair[0])], writes=[tXP])
        add("dve", I("tensor_copy", out=XP[:, 0:3], in_=convc[:, :, lb]), reads=["convc"], writes=[tXP])
        if samples:
            add("act", I("copy", out=XS[:, 0:NS], in_=pb[pair[1]][:, 0:NS]), reads=[pbt(pair[1])], writes=[tXS])
            add("act", I("copy", out=CONVS[:, lb, :], in_=XS[:, 0:NS]), reads=[tXS], writes=["CONVS"])
        add("dve", I("tensor_scalar", out=XC[:, 0:PT], in0=XP[:, 3:3 + PT], scalar1=pv(CW0, 3 * 8 + lb),
                     scalar2=pv(CB0, lb), op0=ALU.mult, op1=ALU.add), reads=[tXP, "pvec"], writes=[tXC])
        for j in range(3):
            add("dve", I("scalar_tensor_tensor", out=XC[:, 0:PT], in0=XP[:, j:j + PT], scalar=pv(CW0, j * 8 + lb),
                         in1=XC[:, 0:PT], op0=ALU.mult, op1=ALU.add), reads=[tXP, tXC, "pvec"], writes=[tXC])
        add("dve", I("tensor_copy", out=convc[:, :, lb], in_=XP[:, PT:PT + 3]), reads=[tXP], writes=["convc"])
        yield
        if samples:
            add("dve", I("tensor_scalar", out=XC[:, PT:PT + NS], in0=XS[:, 0:NS], scalar1=pv(CW0, 3 * 8 + lb),
                         scalar2=pv(CB0, lb), op0=ALU.mult, op1=ALU.add), reads=[tXS, "pvec"], writes=[tXC])
            for j in range(3):
                add("dve", I("scalar_tensor_tensor", out=XC[:, PT:PT + NS], in0=convT[:, lb, j, :],
                             scalar=pv(CW0, j * 8 + lb), in1=XC[:, PT:PT + NS], op0=ALU.mult, op1=ALU.add),
                    reads=["convT", tXC, "pvec"], writes=[tXC])
        add("act", I("copy", out=XCB[:, 0:ncol], in_=XC[:, 0:ncol]), reads=[tXC], writes=[tXCB])
        yield
        pX = pairr.next()
        pA = pairr.next()
        for (pr, wt_, wtok) in [(pX, GXW, "GXW"), (pA, GAW, "GAW")]:
            for (b, pc, n, tc) in cols(pr, samples):
                add("pe", I("matmul", pb[b][:, pc:pc + n], lhsT=wt_[:, lb, :], rhs=XCB[:, tc:tc + n], start=True,
                            stop=True), reads=[wtok, tXCB], writes=[pbt(b)])
        for (b, pc, n, tc) in cols(pX, samples):
            add("act", I("activation", out=GX[:, tc:tc + n], in_=pb[b][:, pc:pc + n], func=AF.Sigmoid,
                         bias=pv(GXB0, lb)), reads=[pbt(b), "pvec"], writes=[tGX])
        for (b, pc, n, tc) in cols(pA, samples):
            add("act", I("activation", out=GA[:, tc:tc + n], in_=pb[b][:, pc:pc + n], func=AF.Sigmoid,
                         bias=pv(GAB0, lb)), reads=[pbt(b), "pvec"], writes=[tGA])
        yield
        add("dve", I("tensor_tensor", out=GX[:, 0:ncol], in0=GX[:, 0:ncol], in1=XC[:, 0:ncol], op=ALU.mult),
            reads=[tGX, tXC], writes=[tGX])
        add("act", I("activation", out=A_[:, 0:ncol], in_=GA[:, 0:ncol], func=AF.Exp, scale=nsp[:, lb:lb + 1]),
            reads=[tGA, "nsp"], writes=[tA])
        add("act", I("activation", out=E2t[:, 0:ncol], in_=GA[:, 0:ncol], func=AF.Exp, scale=nsp2[:, lb:lb + 1]),
            reads=[tGA, "nsp2"], writes=[tE2t])
        add("act", I("activation", out=E2t[:, 0:ncol], in_=E2t[:, 0:ncol], func=AF.Ln, scale=-1.0,
                     bias=ones_c[:, 0:1]), reads=[tE2t, "ones_c"], writes=[tE2t])
        add("act", I("activation", out=MU[:, 0:ncol], in_=E2t[:, 0:ncol], func=AF.Exp, scale=0.5), reads=[tE2t],
            writes=[tMU])
        yield
        add("dve", I("tensor_tensor", out=MU[:, 0:ncol], in0=MU[:, 0:ncol], in1=GX[:, 0:ncol], op=ALU.mult),
            reads=[tMU, tGX], writes=[tMU])
        add("dve", I("tensor_tensor_scan", out=HS[:, 0:PT], data0=A_[:, 0:PT], data1=MU[:, 0:PT],
                     initial=hcar[:, lb:lb + 1], op0=ALU.mult, op1=ALU.add), reads=[tA, tMU, "hcar"],
            writes=[tHS])
        add("dve", I("tensor_copy", out=hcar[:, lb:lb + 1], in_=HS[:, PT - 1:PT]), reads=[tHS], writes=["hcar"])
        yield
        if samples:
            add("dve", I("tensor_tensor", out=HS[:, PT:PT + NS], in0=A_[:, PT:PT + NS], in1=hlruT[:, lb, :],
                         op=ALU.mult), reads=[tA, "hlruT", tHS], writes=[tHS])
            add("dve", I("tensor_tensor", out=HS[:, PT:PT + NS], in0=HS[:, PT:PT + NS], in1=MU[:, PT:PT + NS],
                         op=ALU.add), reads=[tMU, tHS], writes=[tHS])
            add("act", I("copy", out=LRUS[:, lb, :], in_=HS[:, PT:PT + NS]), reads=[tHS], writes=["LRUS"])
        if full:
            pair = zpair.next()
            zmm(slot2, pair, samples, hT, "hT")
            for (b, pc, n, tc) in cols(pair, samples):
                add("act", I("activation", out=SILg[:, tc:tc + n], in_=pb[b][:, pc:pc + n], func=AF.Sigmoid),
                    reads=[pbt(b)], writes=[tSIL])
                add("dve", I("tensor_tensor", out=SILg[:, tc:tc + n], in0=SILg[:, tc:tc + n],
                             in1=pb[b][:, pc:pc + n], op=ALU.mult), reads=[pbt(b), tSIL], writes=[tSIL])
            add("dve", I("tensor_tensor", out=OGT[:, lb, 0:ncol], in0=HS[:, 0:ncol], in1=SILg[:, 0:ncol],
                         op=ALU.mult), reads=[tHS, tSIL], writes=["OGT"])
        yield

    def merge_block(j, ps):
        samples = ps["samples"]
        ncol = PT + (NS if samples else 0)
        SG, M1, M2 = W[13], W[14], W[15]
        p_ = j % 2
        s_r = (ARzP[p_][0], "ARzM%d0" % p_)
        s_g = (ARzP[p_][1], "ARzM%d1" % p_)
        for (st_, wsrc_) in ((s_r, worw_d), (s_g, wolru_d)):
            add("pool", I("dma_start", out=st_[0][:, 0:8, :], in_=wsrc_[j]),
                writes=["AR3p%d" % p_, st_[1]], dma="ab_" + st_[1])
        s_mr = zblock(6272 + j * 128, samples)
        s_mg = zblock(8320 + j * 128, samples)
        for (s_y, s_m, src, stok, Mx, mtok) in [(s_r, s_mr, ORT, "ORT", M1, wt[14]),
                                                (s_g, s_mg, OGT, "OGT", M2, wt[15])]:
            pY = zpair.next()
            zmm(s_y, pY, samples, src, stok, kchunks=8)
            pM = pairr.next()
            zmm(s_m, pM, samples, hT, "hT")
            for (b, pc, n, tc) in cols(pM, samples):
                add("act", I("activation", out=SG[:, tc:tc + n], in_=pb[b][:, pc:pc + n], func=AF.Sigmoid),
                    reads=[pbt(b)], writes=[wt[13]])
            for (b, pc, n, tc) in cols(pY, samples):
                add("dve", I("tensor_tensor", out=Mx[:, tc:tc + n], in0=pb[b][:, pc:pc + n], in1=SG[:, tc:tc + n],
                             op=ALU.mult), reads=[pbt(b), wt[13]], writes=[mtok])
        add("dve", I("tensor_tensor", out=MT[:, j, 0:ncol], in0=M1[:, 0:ncol], in1=M2[:, 0:ncol], op=ALU.add),
            reads=[wt[14], wt[15]], writes=["ARENA1"])

    def final_stage(ps):
        samples = ps["samples"]
        src = xsrc[ps["src"]]
        row0 = ps["row0"]
        tiles_all = [(tt, 128) for tt in range(4)] + ([(4, NS)] if samples else [])
        FG, fgt = OGT[:].rearrange("p a b -> p (a b)").bitcast(F32)[:, 0:D], ["OGT"]
        OUT4 = ORT[:].rearrange("p a b -> p (a b)").bitcast(F32)[:, 0:D]
        JUNK = AR1[:, 0:D]
        add("sp", I("dma_start", out=FG, in_=fng_d.partition_broadcast(128)), writes=fgt, dma="FG")
        OUTB, otk = {}, {}
        for (t, m) in tiles_all:
            if t < 4:
                OUTB[t] = WA[:, t * D:(t + 1) * D]
                otk[t] = wtoks(t * D, (t + 1) * D)
                add("sp", I("dma_start", out=OUTB[t][:, :], in_=src[row0 + t * 128:row0 + (t + 1) * 128, :]),
                    writes=otk[t], dma="OUTB%d" % t)
            else:
                OUTB[t] = OUT4
                otk[t] = ["ORT"]
                add("sp", I("dma_start", out=OUTB[t][0:NS, :], in_=xs_d[:, :]), writes=otk[t], dma="OUTB%d" % t)
        for cg in range(4):
            for q4 in range(4):
                slot = wrot.next()
                wv = WB[slot][:].rearrange("p a b -> p (a b)").rearrange("p (q c) -> p q c", c=512)
                add("pool", I("dma_start", out=wv, in_=wout_d[cg, q4]),
                    writes=["WB%d" % slot], dma="WB%d" % slot)
                for (t, m) in tiles_all:
                    c0 = t * 128
                    for jj in range(4):
                        j = q4 * 4 + jj
                        add("pe", I("matmul", pb[t][0:m, 0:512], lhsT=MT[:, j, c0:c0 + m], rhs=wv[:, jj, :],
                                    start=(j == 0), stop=(j == 15)), reads=["ARENA1", "WB%d" % slot],
                            writes=[pbt(t)])
            for (t, m) in tiles_all:
                add("dve", I("tensor_tensor", out=OUTB[t][0:m, cg * 512:(cg + 1) * 512], in0=pb[t][0:m, 0:512],
                             in1=OUTB[t][0:m, cg * 512:(cg + 1) * 512], op=ALU.add),
                    reads=[pbt(t)] + otk[t], writes=otk[t])
        for (t, m) in tiles_all:
            add("act", I("activation", out=JUNK[0:m, :], in_=OUTB[t][0:m, :], func=AF.Square,
                         accum_out=ssq[0:m, t:t + 1]), reads=otk[t], writes=["ARENA1", "ssq"])
            add("dve", I("tensor_scalar", out=tq1[0:m, t:t + 1], in0=ssq[0:m, t:t + 1], scalar1=1.0 / D,
                         scalar2=1e-6, op0=ALU.mult, op1=ALU.add), reads=["ssq"], writes=["tq1"])
            add("act", I("activation", out=tq2[0:m, t:t + 1], in_=tq1[0:m, t:t + 1], func=AF.Ln),
                reads=["tq1"], writes=["tq2"])
            add("act", I("activation", out=rstd[0:m, t:t + 1], in_=tq2[0:m, t:t + 1], func=AF.Exp, scale=-0.5),
                reads=["tq2"], writes=["rstd"])
            add("dve", I("scalar_tensor_tensor", out=OUTB[t][0:m, :], in0=OUTB[t][0:m, :],
                         scalar=rstd[0:m, t:t + 1], in1=FG[0:m, :], op0=ALU.mult, op1=ALU.mult),
                reads=otk[t] + ["rstd"] + fgt, writes=otk[t])
            if t < 4:
                r0 = ps["orow0"] + t * 128
                add("sp", I("dma_start", out=yp_d[r0:r0 + 128, :], in_=OUTB[t][:, :]), reads=otk[t],
                    writes=["o_yp%d_%d" % (ps["idx"], t)], dma="o_yp%d" % t)
                out_tokens.append("o_yp%d_%d" % (ps["idx"], t))
            else:
                add("sp", I("dma_start", out=ys_d[:, :], in_=OUTB[t][0:NS, :]), reads=otk[t], writes=["o_ys"],
                    dma="o_ys")
                out_tokens.append("o_ys")

    for ps in passes:
        samples, full = ps["samples"], ps["full"]
        ncol = PT + (NS if samples else 0)
        if ps.get("mask_before"):
            for (t_, tok) in [(Hf[:].rearrange("p a b -> p (a b)"), "Hf"), (shc[:], "shc"),
                              (convc[:].rearrange("p a b -> p (a b)"), "convc"), (hcar[:], "hcar")]:
                add("dve", I("tensor_scalar", out=t_, in0=t_, scalar1=flag[:, 0:1], scalar2=None, op0=ALU.mult),
                    reads=[tok, "flag"], writes=[tok])
        for p_ in range(2):
            for h2_ in range(2):
                o_ = slice((1 - h2_) * 64, (2 - h2_) * 64)
                add("dve", I("memset", ARzP[p_][h2_][o_, :, :], 0.0),
                    reads=[], writes=["AR3p%d" % p_, "ARzM%d%d" % (p_, h2_)])
        MARKS.append((len(P.ops), "pass%d start" % ps["idx"]))
        stage_x(ps)
        MARKS.append((len(P.ops), "after stage_x"))
        s24 = zblock(3072, samples)
        pair = zpair.next()
        zmm(s24, pair, samples, hT, "hT")
        run_g(shift_block(24, pair, ps, W[2], wt[2]))
        add("act", I("activation", out=LWD[0:64, 0:ncol], in_=W[2][0:64, 0:ncol], func=AF.Tanh),
            reads=[wt[2], "LWt"], writes=["LWt"])
        add("act", I("copy", out=LWI[64:128, 0:ncol], in_=W[2][64:128, 0:ncol]), reads=[wt[2], "LWt"],
            writes=["LWt"])
        run_g(part1(0, ps))
        part2_build(0, ps)
        interleave((part2(0, ps), 1), (part1(1, ps), 1))
        for hp in range(8):
            MARKS.append((len(P.ops), "wkv hp%d" % hp))
            if hp + 1 < 8:
                part2_build(hp + 1, ps)
            interleave((chain(hp, ps), 1),
                       (part2(hp + 1, ps) if hp + 1 < 8 else None, 2),
                       (part1(hp + 2, ps) if hp + 2 < 8 else None, 2),
                       (post(hp - 1, ps) if hp >= 1 else None, 1))
            if full:
                add("act", I("copy", out=YT[:, 0:PT], in_=pb[BY][:, 0:PT]), reads=[pbt(BY)], writes=[tYT])
        run_g(post(7, ps))
        for lb in range(0, 8, 2):
            MARKS.append((len(P.ops), "lru lb%d" % lb))
            interleave((lru_block(lb, ps, 0), 1), (lru_block(lb + 1, ps, 1), 1))
        MARKS.append((len(P.ops), "after lru"))
        if full:
            for j in range(16):
                merge_block(j, ps)
            final_stage(ps)

    def out_T(src_ap, n, dst_ap, tok, src_tok):
        b = nbr.next()
        add("pe", I("transpose", pb[b][0:n, 0:128], src_ap, ident_f[:]), reads=[src_tok, "ident_f"],
            writes=[pbt(b)])
        add("dve", I("tensor_copy", out=SMALLO[0:n, :], in_=pb[b][0:n, 0:128]), reads=[pbt(b)], writes=[wt[0]])
        add("sp", I("dma_start", out=dst_ap, in_=SMALLO[0:n, :]), reads=[wt[0]], writes=[tok], dma=tok)
        out_tokens.append(tok)

    out_T(shc[:], 25, o_shp[:, :], "o_shp", "shc")
    out_T(hcar[:], 8, o_lrup[:, :], "o_lrup", "hcar")
    for j in range(3):
        out_T(convc[:, j, :], 8, o_convp[j], "o_convp%d" % j, "convc")
    for hp in range(8):
        b = nbr.next()
        add("pe", I("transpose", pb[b][0:64, 0:128], Hf[:, hp, :], ident_f[:]), reads=["Hf", "ident_f"],
            writes=[pbt(b)])
        add("dve", I("tensor_copy", out=WKVO[:, hp, :], in_=pb[b][0:64, 0:128]), reads=[pbt(b)], writes=[wt[0], wt[1]])
    for hp in range(8):
        add("sp", I("dma_start", out=o_wkvp[2 * hp:2 * hp + 2].rearrange("h v k -> v h k"),
                    in_=WKVO[:, hp, :].rearrange("p (h k) -> p h k", k=64)), reads=[wt[0], wt[1]],
            writes=["o_wkvp%d" % hp], dma="o_wkvp")
        out_tokens.append("o_wkvp%d" % hp)
    out_tokens.append("o_wkvs")

    def out_rows(srcT, nblk, dst_ap, tok, src_tok):
        stg2 = WA[0:NS, 0:nblk * 128]
        toks = [wt[i] for i in range(0, (nblk * 128 - 1) // NCOLMAX + 1)]
        for g0 in range(0, nblk, 4):
            b = nbr.next()
            n = min(4, nblk - g0)
            for q in range(n):
                add("pe", I("transpose", pb[b][0:NS, q * 128:(q + 1) * 128], srcT[:, g0 + q, :], ident_f[:]),
                    reads=[src_tok, "ident_f"], writes=[pbt(b)])
            add("dve", I("tensor_copy", out=stg2[:, g0 * 128:(g0 + n) * 128], in_=pb[b][0:NS, 0:n * 128]),
                reads=[pbt(b)], writes=toks)
        add("sp", I("dma_start", out=dst_ap, in_=stg2), reads=toks, writes=[tok], dma=tok)
        out_tokens.append(tok)

    out_rows(shiftOutT, 25, o_shs[:, :], "o_shs", "shiftOutT")
    out_rows(LRUS, 8, o_lrus[:, :], "o_lrus", "LRUS")
    out_rows(CONVS, 8, o_convs[:, 2, :], "o_convs2", "CONVS")
    add("sp", I("dma_start", out=o_convs[:, 0:2, :], in_=sconv_d[:, 1:3, :]), writes=["o_convs01"], dma="o_convs01")
    out_tokens.append("o_convs01")
    add("sp", None, reads=list(dict.fromkeys(out_tokens)))
    stats = P.emit()
    return nc, stats


PASSES = [
    dict(idx=0, src="xprev", row0=0, full=False, samples=False, rmode="none"),
    dict(idx=1, src="xprev", row0=512, full=False, samples=False, rmode="last"),
    dict(idx=2, src="xown", row0=0, orow0=0, full=True, samples=False, mask_before=True),
    dict(idx=3, src="xown", row0=512, orow0=512, full=True, samples=True),
]

_CACHE = {}


def _pvec(inp):
    def colz(v):
        v = np.asarray(v, np.float32).reshape(-1)
        return v.reshape(-1, 128).T
    parts = [colz(inp["norm_g"][0]), colz(inp["rwkv_mu"][0]), colz(inp["w_decay0"][0]), colz(inp["w_iclr0"][0]),
             colz(inp["k_k"][0]), colz(inp["k_a"][0]), colz(inp["r_k"][0]), colz(inp["ln_x_g"][0]),
             colz(inp["ln_x_b"][0]), colz(inp["conv_w"][0]), colz(inp["conv_b"][0]), colz(inp["lru_gx_b"][0]),
             colz(inp["lru_ga_b"][0]), colz(inp["lru_lambda"][0])]
    pv = np.ascontiguousarray(np.concatenate(parts, axis=1), dtype=np.float32)
    assert pv.shape == (128, NPV), pv.shape
    return pv


def kernel(**inp):
    if "nc" not in _CACHE:
        _CACHE["nc"] = build(PASSES)
    nc, stats = _CACHE["nc"]
    f = lambda a: np.ascontiguousarray(np.asarray(a, dtype=np.float32))
    xp = f(inp["x_prompt"])
    xsm = f(inp["x_sample"])
    pvec = _pvec(inp)
    shared = dict(
        pvec=pvec,
        w_in=f(np.asarray(inp["w_in"][0], np.float32).reshape(16, 128, INW // 128, 128).transpose(2, 1, 0, 3)),
        wdu=f(inp["w_decay_up"][0]), wiu=f(inp["w_iclr_up"][0]),
        worw=f(np.asarray(inp["w_out_rwkv"][0], np.float32).reshape(8, 128, 16, 128).transpose(2, 1, 0, 3)),
        wolru=f(np.asarray(inp["w_out_lru"][0], np.float32).reshape(8, 128, 16, 128).transpose(2, 1, 0, 3)),
        gxw=f(inp["lru_gx_w"][0]), gaw=f(inp["lru_ga_w"][0]),
        wout=f(np.asarray(inp["w_out"][0], np.float32).reshape(4, 4, 128, 4, 512).transpose(3, 0, 2, 1, 4)),
        fng=f(inp["final_norm_g"]).reshape(1, D))
    in_maps = []
    for c in range(NCORES):
        b, half = c // 2, c % 2
        m = dict(shared)
        m["xown"] = f(xp[b, half * 1024:(half + 1) * 1024])
        m["xprev"] = f(xp[b, 0:1024])
        m["flag"] = np.full((128, 1), float(half), np.float32)
        sl = slice(c * NS, (c + 1) * NS)
        m["xs"] = f(xsm[sl, 0])
        m["sshift"] = f(inp["state_shift"][0, sl])
        m["swkv"] = f(inp["state_wkv"][0, sl])
        m["sconv"] = f(inp["state_conv"][0, sl])
        m["slru"] = f(inp["state_lru"][0, sl])
        in_maps.append(m)
    res = run_bass_kernel_spmd(nc, in_maps, core_ids=list(range(NCORES)))
    R = res.results
    y_prompt = np.stack([np.concatenate([R[2 * b]["yp"], R[2 * b + 1]["yp"]], axis=0) for b in range(4)])
    y_sample = np.concatenate([R[c]["ys"] for c in range(NCORES)], axis=0)[:, None, :]
    odd = [R[2 * b + 1] for b in range(4)]
    nsh_p = np.stack([o["o_shp"].reshape(SHW) for o in odd])[None]
    nwkv_p = np.stack([o["o_wkvp"] for o in odd])[None]
    nconv_p = np.stack([o["o_convp"].reshape(3, 1024) for o in odd])[None]
    nlru_p = np.stack([o["o_lrup"].reshape(1024) for o in odd])[None]
    nsh_s = np.concatenate([R[c]["o_shs"] for c in range(NCORES)], axis=0)[None]
    nwkv_s = np.concatenate([R[c]["o_wkvs"] for c in range(NCORES)], axis=0)[None]
    nconv_s = np.concatenate([R[c]["o_convs"] for c in range(NCORES)], axis=0)[None]
    nlru_s = np.concatenate([R[c]["o_lrus"] for c in range(NCORES)], axis=0)[None]
    outs = (y_prompt, y_sample, nsh_p, nwkv_p, nconv_p, nlru_p, nsh_s, nwkv_s, nconv_s, nlru_s)
    return tuple(np.ascontiguousarray(o, dtype=np.float32) for o in outs)
```

```python
import numpy as np
import concourse.bass as bass
import concourse.mybir as mybir
from concourse.bass_utils import run_bass_kernel_spmd

F32 = mybir.dt.float32
BF16 = mybir.dt.bfloat16
AF = mybir.ActivationFunctionType
ALU = mybir.AluOpType
AX = mybir.AxisListType

EPOCH = 6000
MAXOPS = None
MARKS = []
NCORES = 8
D = 2048
CW = 1024
INW = 10368
SHW = 3200
PT = 512
NS = 16
NCH = 8
G0, MU0, WD0, WI0, KK0, KA0, RK0, LG0, LB0, CW0, CB0, GXB0, GAB0, LAM0, NPV = (
    0, 16, 41, 49, 57, 65, 73, 81, 89, 97, 129, 137, 145, 153, 161)


class Prog:
    def __init__(self, nc):
        self.nc = nc
        self.ops = []
        self.res = {}
        self.eng = {"pe": nc.tensor, "act": nc.scalar, "dve": nc.vector,
                    "pool": nc.gpsimd, "sp": nc.sync}
        self.dma_count = {}
        self.nsem = 0
        self.fin = []
        self.eng_free = {e: 0.0 for e in self.eng}

    def add(self, eng, fn, reads=(), writes=(), dma=None):
        i = len(self.ops)
        if MAXOPS is not None and i >= MAXOPS:
            self.fin.append(0.0)
            return i
        deps = set()
        for t in reads:
            st = self.res.get(t)
            if st is not None and st[0] is not None:
                deps.add(st[0])
        for t in writes:
            st = self.res.get(t)
            if st is not None:
                if st[0] is not None:
                    deps.add(st[0])
                deps.update(st[1].values())
                deps.update(st[2])
        for t in reads:
            st = self.res.get(t)
            if st is None:
                st = [None, {}, []]
                self.res[t] = st
            if dma is not None:
                st[2].append(i)
            else:
                st[1][eng] = i
        for t in writes:
            self.res[t] = [i, {}, []]
        deps.discard(i)
        dn = None
        if dma is not None:
            dn = self.dma_count.get(dma, 0) + 1
            self.dma_count[dma] = dn
        self.ops.append(dict(eng=eng, fn=fn, deps=deps, dma=dma, dn=dn))
        st_ = self.eng_free[eng]
        for d_ in deps:
            if self.fin[d_] + 0.2 > st_:
                st_ = self.fin[d_] + 0.2
        du_ = _est(eng, fn, dma) if fn is not None else 0.0
        if dma is not None:
            self.eng_free[eng] = st_ + (1.2 if eng == "pool" else 0.1)
        else:
            self.eng_free[eng] = st_ + du_
        self.fin.append(st_ + du_)
        return i

    def _sem(self, name):
        self.nsem += 1
        return self.nc.alloc_semaphore(name=name)

    def emit(self):
        ops = self.ops
        seen = {e: {} for e in self.eng}
        snaps = [None] * len(ops)
        waits = [None] * len(ops)
        need_sig = [False] * len(ops)
        for i, op in enumerate(ops):
            E = op["eng"]
            s = seen[E]
            wl = []
            dl = sorted(op["deps"])
            mx = {}
            for d in dl:
                if ops[d]["dma"] is not None:
                    mx[ops[d]["dma"]] = d
            for d in dl:
                p = ops[d]
                if p["dma"] is not None and mx[p["dma"]] != d:
                    continue
                if p["dma"] is not None:
                    k = ("dma", p["dma"])
                    if s.get(k, 0) >= p["dn"]:
                        continue
                    wl.append(d)
                    for kk, vv in snaps[d].items():
                        if s.get(kk, -1) < vv:
                            s[kk] = vv
                    s[k] = p["dn"]
                else:
                    Fe = p["eng"]
                    if Fe == "pe" and E == "pe":
                        continue
                    if s.get(Fe, -1) >= d:
                        continue
                    wl.append(d)
                    need_sig[d] = True
                    for kk, vv in snaps[d].items():
                        if s.get(kk, -1) < vv:
                            s[kk] = vv
                    s[Fe] = d
            waits[i] = wl
            snaps[i] = dict(s)
        sig = {}
        ecount = {e: 0 for e in self.eng}
        esem = {e: [] for e in self.eng}
        dsem = {}
        nwait = 0
        for i, op in enumerate(ops):
            E = op["eng"]
            h = self.eng[E]
            for d in waits[i]:
                sm, val = sig[d]
                h.wait_ge(sm, val)
                nwait += 1
            inst = op["fn"](h) if op["fn"] is not None else None
            if op["dma"] is not None:
                key = op["dma"]
                if key not in dsem:
                    dsem[key] = self._sem("d_" + str(key))
                inst.then_inc(dsem[key], 16)
                sig[i] = (dsem[key], 16 * op["dn"])
            elif need_sig[i]:
                c = ecount[E]
                ep = c // EPOCH
                while len(esem[E]) <= ep:
                    esem[E].append(self._sem("e_%s_%d" % (E, len(esem[E]))))
                inst.then_inc(esem[E][ep], 1)
                ecount[E] = c + 1
                sig[i] = (esem[E][ep], c % EPOCH + 1)
        self.stats = dict(nops=len(ops), nwait=nwait, nsem=self.nsem, sigs=dict(ecount))
        return self.stats


def I(method, *args, **kw):
    f = lambda e: getattr(e, method)(*args, **kw)
    f.meta = (method, args, kw)
    return f


def _est(eng, fn, dma):
    n = 256
    try:
        method, args, kw = fn.meta
        o = kw.get("out", args[0] if args else None)
        n = 1
        for d_ in o.shape[1:]:
            n *= int(d_)
    except Exception:
        method = ""
    if dma is not None:
        return 5.0
    if eng == "pe":
        return 0.07 + n / 1400.0
    if eng == "act":
        return 0.2 + n / 1200.0
    if eng == "dve":
        return (0.15 + n / 480.0) if method == "tensor_tensor_scan" else (0.15 + n / 960.0)
    return 0.5 + n / 500.0


class Rot:
    def __init__(self, items):
        self.items = list(items)
        self.i = 0

    def next(self):
        v = self.items[self.i % len(self.items)]
        self.i += 1
        return v


def build(passes):
    nc = bass.Bass("TRN2", target_bir_lowering=False)
    P = Prog(nc)
    add = P.add

    def din(name, shape):
        return nc.dram_tensor(name, shape, F32, kind="ExternalInput").ap()

    def dout(name, shape):
        return nc.dram_tensor(name, shape, F32, kind="ExternalOutput").ap()

    xsrc = {"xprev": din("xprev", [1024, D]), "xown": din("xown", [1024, D])}
    xs_d = din("xs", [NS, D])
    flag_d = din("flag", [128, 1])
    sshift_d = din("sshift", [NS, SHW])
    swkv_d = din("swkv", [NS, 16, 64, 64])
    sconv_d = din("sconv", [NS, 3, 1024])
    slru_d = din("slru", [NS, 1024])
    pvec_d = din("pvec", [128, NPV])
    w_in = din("w_in", [INW // 128, 128, 16, 128])
    wdu_d = din("wdu", [64, CW])
    wiu_d = din("wiu", [64, CW])
    worw_d = din("worw", [16, 128, 8, 128])
    wolru_d = din("wolru", [16, 128, 8, 128])
    gxw_d = din("gxw", [16, 64, 64])
    gaw_d = din("gaw", [16, 64, 64])
    wout_d = din("wout", [4, 4, 128, 4, 512])
    fng_d = din("fng", [1, D])
    yp_d = dout("yp", [1024, D])
    ys_d = dout("ys", [NS, D])
    o_shp = dout("o_shp", [25, 128])
    o_wkvp = dout("o_wkvp", [16, 64, 64])
    o_convp = dout("o_convp", [3, 8, 128])
    o_lrup = dout("o_lrup", [8, 128])
    o_shs = dout("o_shs", [NS, SHW])
    o_wkvs = dout("o_wkvs", [NS, 16, 64, 64])
    o_convs = dout("o_convs", [NS, 3, 1024])
    o_lrus = dout("o_lrus", [NS, 1024])
    out_tokens = []

    def sb(name, shape, dt):
        return nc.sbuf_tensor("s_" + name, shape, dt).__enter__()

    pb = [nc.psum_tensor("pb%d" % i, [128, 512], F32).__enter__() for i in range(8)]
    zpair = Rot([(0, 1), (2, 3)])
    pairr = Rot([((5, 7)[i % 2], 6, (i % 32) * NS) for i in range(32)])
    nbr = Rot([5, 6, 7])
    nbr2 = Rot([4, 5, 6, 7])

    def pbt(b):
        return "pb%d" % b

    ident_f = sb("ident_f", [128, 128], F32)
    ident_b = sb("ident_b", [128, 128], BF16)
    onesblk = sb("onesblk", [128, 128], F32)
    E2 = sb("E2", [128, 64], F32)
    ones_b = sb("ones_b", [128, 128], BF16)
    I16 = sb("I16", [64, 64], BF16)
    mask4 = sb("mask4", [64, 4, 128], BF16)
    maskY = sb("maskY", [64, 8, 64], BF16)
    segmask = sb("segmask", [128, 512], BF16)
    pvec = sb("pvec", [128, NPV], F32)
    omka = sb("omka", [128, 8], F32)
    nsp = sb("nsp", [128, 8], F32)
    nsp2 = sb("nsp2", [128, 8], F32)
    flag = sb("flag", [128, 1], F32)
    ones_c = sb("ones_c", [128, 1], F32)
    epsc = sb("epsc", [128, 2], F32)
    lora_w = sb("lora_w", [128, CW], BF16)
    GXW = sb("GXW", [128, 8, 128], BF16)
    GAW = sb("GAW", [128, 8, 128], BF16)
    shiftT = sb("shiftT", [128, 25, NS], F32)
    convT = sb("convT", [128, 8, 3, NS], F32)
    hlruT = sb("hlruT", [128, 8, NS], F32)
    shiftOutT = sb("shiftOutT", [128, 25, NS], F32)
    LRUS = sb("LRUS", [128, 8, NS], F32)
    CONVS = sb("CONVS", [128, 8, NS], F32)
    NCOLMAX = PT + NS
    AR1 = sb("ARENA1", [128, 16 * NCOLMAX], BF16)
    xn = AR1[:, 0:4 * D].rearrange("p (t d) -> p t d", t=4)
    MT = AR1[:, 0:16 * NCOLMAX].rearrange("p (j c) -> p j c", j=16)
    hT = sb("hT", [128, 16, NCOLMAX], BF16)
    WB = [sb("WB%d" % i, [128, 16, 128], BF16) for i in range(4)]
    wrot = Rot(range(4))
    NW = 22
    WA = sb("WA", [128, NW * NCOLMAX], F32)
    W = [WA[:, i * NCOLMAX:(i + 1) * NCOLMAX] for i in range(NW)]
    wt = ["W%d" % i for i in range(NW)]

    def wtoks(c0, c1):
        return [wt[i] for i in range(c0 // NCOLMAX, (c1 - 1) // NCOLMAX + 1)]
    XT = [WA[:, 0:D], WA[:, D:2 * D]]
    xtt = [wtoks(0, D), wtoks(D, 2 * D)]
    LWD = sb("LWD", [128, NCOLMAX], BF16)
    LWI = sb("LWI", [128, NCOLMAX], BF16)
    ARzP = [[sb("ARz%d_%d" % (p_, i), [128, NCH, 128], BF16) for i in range(2)] for p_ in range(2)]
    KBz = [sb("KBz%d" % i, [128, NCH, 128], BF16) for i in range(2)]
    KB3 = sb("KB3", [128, NCH, 128], BF16)
    VB = sb("VB", [128, PT], BF16)
    TOKP = [sb("TOK_%d" % p_, [64, 3, NCH, 128], BF16) for p_ in range(2)]
    MSKP = [sb("MSK_%d" % p_, [64, 16, 128], BF16) for p_ in range(2)]
    MSBP = [sb("MSB_%d" % p_, [64, 16, 128], BF16) for p_ in range(2)]
    TTbP = [sb("TTb_%d" % p_, [64, 16, 64], BF16) for p_ in range(2)]
    BONb = [sb("BONb%d" % i, [128, NCOLMAX], BF16) for i in range(4)]
    XA = sb("XA", [64, 16, 64], BF16)
    YA = sb("YA", [64, 16, 64], BF16)
    XBt = sb("XBt", [64, 16, 64], BF16)
    YBt = sb("YBt", [64, 16, 64], BF16)
    Y1 = YBt
    QA = sb("QA", [64, 16, 64], BF16)
    QB = sb("QB", [64, 16, 64], BF16)
    WS = sb("WS", [64, 128], BF16)
    US = sb("US", [64, NCH, 128], BF16)
    Hball = sb("Hball", [128, NCH + 1, 64], BF16)
    XCB0 = sb("XCB", [128, NCOLMAX], BF16)
    Hf = sb("Hf", [128, 8, 64], F32)
    HDL = sb("HDL", [128, 64], F32)
    SWt = sb("SWt", [128, NS, 64], F32)
    shc = sb("shc", [128, 25], F32)
    convc = sb("convc", [128, 3, 8], F32)
    hcar = sb("hcar", [128, 8], F32)
    ssq = sb("ssq", [128, 8], F32)
    tq1 = sb("tq1", [128, 8], F32)
    tq2 = sb("tq2", [128, 8], F32)
    rstd = sb("rstd", [128, 8], F32)
    DLtP = [sb("DLt%d" % p_, [128, NCH], F32) for p_ in range(2)]
    SMP2 = sb("SMP", [128, 3, 6, NS], F32)
    omu = sb("omu", [128, 25], F32)
    nkk = sb("nkk", [128, 8], F32)
    SA = sb("SA", [128, NS], F32)
    ORT = sb("ORT", [128, 8, NCOLMAX], BF16)
    xns = ORT[:].rearrange("p a b -> p (a b)")[:, 0:D]
    OGT = sb("OGT", [128, 8, NCOLMAX], BF16)
    WKVO = WA[0:64, 0:1024].rearrange("p (a b) -> p a b", b=128)
    SMALLO = WA[0:32, 0:128]

    def pv(off, i):
        return pvec[:, off + i:off + i + 1]

    add("sp", I("dma_start", out=pvec[:], in_=pvec_d[:, :]), writes=["pvec"], dma="pvec")
    add("sp", I("dma_start", out=flag[:], in_=flag_d[:, :]), writes=["flag"], dma="flag")
    add("pool", I("dma_start", out=lora_w[0:64, :], in_=wdu_d[:, :]), writes=["lora_w0"], dma="lw0")
    add("pool", I("dma_start", out=lora_w[64:128, :], in_=wiu_d[:, :]), writes=["lora_w1"], dma="lw1")
    add("dve", I("memset", GXW[:], 0.0), writes=["GXW"])
    add("dve", I("memset", GAW[:], 0.0), writes=["GAW"])
    for h2 in range(2):
        src = gxw_d.rearrange("(hb h2) i j -> h2 i hb j", h2=2)[h2]
        add("pool", I("dma_start", out=GXW[h2 * 64:(h2 + 1) * 64, :, h2 * 64:(h2 + 1) * 64], in_=src),
            reads=[], writes=["GXW"], dma="gxw")
        src = gaw_d.rearrange("(hb h2) i j -> h2 i hb j", h2=2)[h2]
        add("pool", I("dma_start", out=GAW[h2 * 64:(h2 + 1) * 64, :, h2 * 64:(h2 + 1) * 64], in_=src),
            reads=[], writes=["GAW"], dma="gaw")
    add("pool", I("memset", ident_f[:], 0.0), writes=["ident_f"])
    add("pool", I("affine_select", out=ident_f[:], in_=ident_f[:], pattern=[[-1, 128]],
                  compare_op=ALU.not_equal, fill=1.0, base=0, channel_multiplier=1),
        reads=["ident_f"], writes=["ident_f"])
    add("dve", I("tensor_copy", out=ident_b[:], in_=ident_f[:]), reads=["ident_f"], writes=["ident_b"])
    add("dve", I("memset", onesblk[:], 0.0), writes=["onesblk"])
    add("dve", I("memset", onesblk[0:64, 0:64], 1.0), reads=["onesblk"], writes=["onesblk"])
    add("dve", I("memset", onesblk[64:128, 64:128], 1.0), reads=["onesblk"], writes=["onesblk"])
    add("dve", I("tensor_copy", out=ones_b[:], in_=onesblk[:]), reads=["onesblk"], writes=["ones_b"])
    add("dve", I("tensor_tensor", out=E2[:], in0=ident_f[:, 0:64], in1=ident_f[:, 64:128], op=ALU.add),
        reads=["ident_f"], writes=["E2"])
    add("dve", I("tensor_copy", out=I16[:], in_=ident_f[0:64, 0:64]), reads=["ident_f"], writes=["I16"])
    add("dve", I("memset", LWD[:], 0.0), writes=["LWt"])
    add("dve", I("memset", LWI[:], 0.0), writes=["LWt"])
    add("pool", I("memset", mask4[:], 1.0), writes=["mask4"])
    for q in range(4):
        add("pool", I("affine_select", out=mask4[:, q, 0:64], in_=mask4[:, q, 0:64], pattern=[[1, 64]],
                      compare_op=ALU.is_gt, fill=0.0, base=0, channel_multiplier=-1),
            reads=["mask4"], writes=["mask4"])
        add("pool", I("affine_select", out=mask4[:, q, 64:128], in_=mask4[:, q, 64:128], pattern=[[1, 64]],
                      compare_op=ALU.is_ge, fill=0.0, base=0, channel_multiplier=-1),
            reads=["mask4"], writes=["mask4"])
    add("pool", I("memset", maskY[:], 1.0), writes=["maskY"])
    for q in range(8):
        add("pool", I("affine_select", out=maskY[:, q, :], in_=maskY[:, q, :], pattern=[[-1, 64]],
                      compare_op=ALU.is_gt, fill=0.0, base=0, channel_multiplier=1),
            reads=["maskY"], writes=["maskY"])
    add("dve", I("memset", ones_c[:], 1.0), writes=["ones_c"])
    add("dve", I("memset", epsc[:, 0:1], 64e-5), writes=["epsc"])
    add("dve", I("memset", epsc[:, 1:2], 1e-6), reads=["epsc"], writes=["epsc"])
    add("dve", I("memset", segmask[:], 1.0), writes=["segmask"])
    add("dve", I("memset", segmask[:].rearrange("p (c l) -> p c l", l=64)[:, :, 0:1], 0.0),
        reads=["segmask"], writes=["segmask"])
    add("dve", I("tensor_scalar", out=omka[:], in0=pvec[:, KA0:KA0 + 8], scalar1=-1.0, scalar2=1.0,
                 op0=ALU.mult, op1=ALU.add), reads=["pvec"], writes=["omka"])
    add("dve", I("tensor_scalar", out=omu[:], in0=pvec[:, MU0:MU0 + 25], scalar1=-1.0, scalar2=1.0,
                 op0=ALU.mult, op1=ALU.add), reads=["pvec"], writes=["omu"])
    add("dve", I("tensor_scalar", out=nkk[:], in0=pvec[:, KK0:KK0 + 8], scalar1=-1.0, scalar2=None,
                 op0=ALU.mult), reads=["pvec"], writes=["nkk"])
    add("act", I("activation", out=tq1[:], in_=pvec[:, LAM0:LAM0 + 8], func=AF.Exp, scale=-1.0),
        reads=["pvec"], writes=["tq1"])
    add("dve", I("tensor_scalar", out=tq2[:], in0=tq1[:], scalar1=1.0, scalar2=None, op0=ALU.add),
        reads=["tq1"], writes=["tq2"])
    add("act", I("activation", out=tq1[:], in_=tq2[:], func=AF.Ln), reads=["tq2"], writes=["tq1"])
    add("dve", I("tensor_scalar", out=nsp[:], in0=tq1[:], scalar1=-8.0, scalar2=None, op0=ALU.mult),
        reads=["tq1"], writes=["nsp"])
    add("dve", I("tensor_scalar", out=nsp2[:], in0=tq1[:], scalar1=-16.0, scalar2=None, op0=ALU.mult),
        reads=["tq1"], writes=["nsp2"])
    for i_ in range(2):
        for p_ in range(2):
            add("dve", I("memset", ARzP[p_][i_][:], 0.0), writes=["AR3p%d" % p_])
        add("dve", I("memset", KBz[i_][:], 0.0), writes=["KBz"])
    add("dve", I("memset", Hf[:], 0.0), writes=["Hf"])
    add("dve", I("memset", shc[:], 0.0), writes=["shc"])
    add("dve", I("memset", convc[:], 0.0), writes=["convc"])
    add("dve", I("memset", hcar[:], 0.0), writes=["hcar"])

    stg = WA[0:NS, 0:SHW]
    stg_tok = wt[0:7]

    def load_T(src_ap, ncols, dst_fn, dst_tok):
        add("sp", I("dma_start", out=stg[:, 0:ncols], in_=src_ap), writes=stg_tok, dma="stg")
        nblk = ncols // 128
        for g0 in range(0, nblk, 16):
            b = nbr.next()
            nb_ = min(16, nblk - g0)
            for q in range(nb_):
                blk = g0 + q
                add("pe", I("transpose", pb[b][:, q * NS:(q + 1) * NS], stg[:, blk * 128:(blk + 1) * 128],
                            ident_f[0:NS, 0:NS]), reads=stg_tok + ["ident_f"], writes=[pbt(b)])
            dst_fn(b, g0, nb_)

    def sh_dst(b, g0, n):
        add("dve", I("tensor_copy", out=shiftT[:, g0:g0 + n, :],
                     in_=pb[b][:, 0:n * NS].rearrange("p (a s) -> p a s", s=NS)),
            reads=[pbt(b)], writes=["shiftT"])

    load_T(sshift_d[:, :], SHW, sh_dst, "shiftT")

    def conv_dst(b, g0, n):
        for q in range(n):
            blk = g0 + q
            j, lb = blk // 8, blk % 8
            add("dve", I("tensor_copy", out=convT[:, lb, j, :], in_=pb[b][:, q * NS:(q + 1) * NS]),
                reads=[pbt(b)], writes=["convT"])

    load_T(sconv_d.rearrange("s j c -> s (j c)"), 3072, conv_dst, "convT")

    def lru_dst(b, g0, n):
        add("dve", I("tensor_copy", out=hlruT[:, g0:g0 + n, :],
                     in_=pb[b][:, 0:n * NS].rearrange("p (a s) -> p a s", s=NS)),
            reads=[pbt(b)], writes=["hlruT"])

    load_T(slru_d[:, :], 1024, lru_dst, "hlruT")

    def cols(pair, samples):
        r = [(pair[0], 0, PT, 0)]
        if samples:
            r.append((pair[1], pair[2] if len(pair) > 2 else 0, NS, PT))
        return r

    def zblock(col0, samples, wsrc=None, kchunks=16):
        slot = wrot.next()
        src = (w_in if wsrc is None else wsrc)[col0 // 128]
        add("pool", I("dma_start", out=WB[slot][:, 0:kchunks, :], in_=src),
            writes=["WB%d" % slot], dma="WB%d" % slot)
        return slot

    def zmm(slot, pair, samples, rhs_tile, rhs_tok, kchunks=16):
        wtile, wtok = (WB[slot], "WB%d" % slot) if isinstance(slot, int) else slot
        for (b, pc, n, tc) in cols(pair, samples):
            for kc in range(kchunks):
                add("pe", I("matmul", pb[b][:, pc:pc + n], lhsT=wtile[:, kc, :],
                            rhs=rhs_tile[:, kc, tc:tc + n], start=(kc == 0), stop=(kc == kchunks - 1)),
                    reads=[wtok, rhs_tok], writes=[pbt(b)])

    def evac(eng, pair, samples, dst, dtok, method="copy", **kw):
        for (b, pc, n, tc) in cols(pair, samples):
            if eng == "act":
                add("act", I("activation", out=dst[:, tc:tc + n], in_=pb[b][:, pc:pc + n], **kw),
                    reads=[pbt(b)] + kw.pop("_r", []) if False else [pbt(b)], writes=[dtok])
            else:
                add("dve", I("tensor_copy", out=dst[:, tc:tc + n], in_=pb[b][:, pc:pc + n]),
                    reads=[pbt(b)], writes=[dtok])

    def stage_x(ps):
        src = xsrc[ps["src"]]
        row0 = ps["row0"]
        samples = ps["samples"]
        tiles = [(tt, 128) for tt in range(4)] + ([(4, NS)] if samples else [])
        for (tt, m) in tiles:
            bx = tt % 2
            if tt < 4:
                add("sp", I("dma_start", out=XT[bx][:, :], in_=src[row0 + tt * 128:row0 + (tt + 1) * 128, :]),
                    writes=xtt[bx], dma="XT%d" % bx)
                dst = xn[:, tt, :]
            else:
                add("sp", I("dma_start", out=XT[bx][0:NS, :], in_=xs_d[:, :]),
                    writes=xtt[bx], dma="XT%d" % bx)
                dst = xns
            add("act", I("activation", out=dst[0:m, :], in_=XT[bx][0:m, :], func=AF.Square,
                         accum_out=ssq[0:m, tt:tt + 1]),
                reads=xtt[bx], writes=["ARENA1", "ORT", "ssq"])
            add("act", I("activation", out=tq2[0:m, tt:tt + 1], in_=ssq[0:m, tt:tt + 1], func=AF.Ln,
                         scale=1.0 / D, bias=epsc[0:m, 1:2]), reads=["ssq", "epsc"], writes=["tq2"])
            add("act", I("activation", out=rstd[0:m, tt:tt + 1], in_=tq2[0:m, tt:tt + 1], func=AF.Exp, scale=-0.5),
                reads=["tq2"], writes=["rstd"])
            add("act", I("activation", out=dst[0:m, :], in_=XT[bx][0:m, :], func=AF.Identity,
                         scale=rstd[0:m, tt:tt + 1]),
                reads=xtt[bx] + ["rstd"], writes=["ARENA1", "ORT"])
        ncol = PT + (NS if samples else 0)
        for k in range(16):
            b = nbr.next()
            pv_ = pb[b][:].bitcast(BF16)
            for (tt, m) in tiles:
                if tt < 4:
                    add("pe", I("transpose", pv_[:, tt * 128:(tt + 1) * 128], xn[:, tt, k * 128:(k + 1) * 128],
                                ident_b[:]), reads=["ARENA1", "ORT", "ident_b"], writes=[pbt(b)])
                else:
                    add("pe", I("transpose", pv_[:, PT:PT + NS], xns[0:NS, k * 128:(k + 1) * 128],
                                ident_b[0:NS, 0:NS]), reads=["ARENA1", "ORT", "ident_b"], writes=[pbt(b)])
            add("dve", I("tensor_scalar", out=hT[:, k, 0:ncol], in0=pv_[:, 0:ncol], scalar1=pv(G0, k),
                         scalar2=None, op0=ALU.mult), reads=[pbt(b), "pvec"], writes=["hT"])

    def shift_block(blk, pair, ps, dst, dtok):
        samples = ps["samples"]
        ncol = PT + (NS if samples else 0)
        Zr = W[0]
        for (b, pc, n, tc) in cols(pair, samples):
            add("act", I("copy", out=Zr[:, tc:tc + n], in_=pb[b][:, pc:pc + n]), reads=[pbt(b)], writes=[wt[0]])
            add("act", I("activation", out=dst[:, tc:tc + n], in_=pb[b][:, pc:pc + n], func=AF.Identity,
                         scale=omu[:, blk:blk + 1]), reads=[pbt(b), "omu"], writes=[dtok])
        yield
        add("dve", I("scalar_tensor_tensor", out=dst[:, 1:PT], in0=Zr[:, 0:PT - 1], scalar=pv(MU0, blk),
                     in1=dst[:, 1:PT], op0=ALU.mult, op1=ALU.add), reads=[wt[0], dtok, "pvec"], writes=[dtok])
        add("dve", I("scalar_tensor_tensor", out=dst[:, 0:1], in0=shc[:, blk:blk + 1], scalar=pv(MU0, blk),
                     in1=dst[:, 0:1], op0=ALU.mult, op1=ALU.add), reads=["shc", dtok, "pvec"], writes=[dtok])
        if samples:
            add("dve", I("scalar_tensor_tensor", out=dst[:, PT:PT + NS], in0=shiftT[:, blk, :], scalar=pv(MU0, blk),
                         in1=dst[:, PT:PT + NS], op0=ALU.mult, op1=ALU.add), reads=["shiftT", dtok, "pvec"],
                writes=[dtok])
            add("act", I("copy", out=shiftOutT[:, blk, :], in_=Zr[:, PT:PT + NS]), reads=[wt[0]],
                writes=["shiftOutT"])
        add("act", I("copy", out=shc[:, blk:blk + 1], in_=Zr[:, PT - 1:PT]), reads=[wt[0], "shc"], writes=["shc"])
        yield

    def ones_mm(src, stok, samples):
        pair = pairr.next()
        for (b, pc, n, tc) in cols(pair, samples):
            add("pe", I("matmul", pb[b][:, pc:pc + n], lhsT=onesblk[:], rhs=src[:, tc:tc + n], start=True,
                        stop=True), reads=["onesblk", stok], writes=[pbt(b)])
        return pair

    def zmm_g(slot, pair, samples, rhs_tile, rhs_tok, kchunks=16):
        for (b, pc, n, tc) in cols(pair, samples):
            for kc in range(kchunks):
                add("pe", I("matmul", pb[b][:, pc:pc + n], lhsT=WB[slot][:, kc, :],
                            rhs=rhs_tile[:, kc, tc:tc + n], start=(kc == 0), stop=(kc == kchunks - 1)),
                    reads=["WB%d" % slot, rhs_tok], writes=[pbt(b)])
                if kc % 4 == 3:
                    yield

    Rt, Kt, Vt, LOGD, AI, AV, KM, BV, CL, EP, EM = (W[3], W[4], W[5], W[6], W[7], W[9], W[10], W[11], W[15],
                                                    W[16], W[17])
    tR, tK, tV, tLOGD, tAI, tAV, tKM, tBV, tCL, tEP, tEM = (wt[3], wt[4], wt[5], wt[6], wt[7], wt[9], wt[10],
                                                            wt[11], wt[15], wt[16], wt[17])
    T1, T2, tT1, tT2 = W[2], W[12], wt[2], wt[12]
    BONs, tBONs = BONb, ["BON%d" % i for i in range(4)]
    YT, T1p, T2p, SILp = W[19], W[8], W[18], W[1]
    tYT, tT1p, tT2p, tSILp = wt[19], wt[8], wt[18], wt[1]
    DSC = 0.6065306597126334
    rows = [slice(0, 64), slice(64, 128)]

    def part1(hp, ps):
        samples, full = ps["samples"], ps["full"]
        ncol = PT + (NS if samples else 0)
        BON, tBON = BONs[hp % 4], tBONs[hp % 4]
        rmode = "full" if full else ps.get("rmode", "full")
        slots = [zblock(hp * 128, samples) if rmode != "none" else None,
                 zblock(1024 + hp * 128, samples), zblock(2048 + hp * 128, samples)]
        yield
        for gi, (dst, dtok) in enumerate([(Rt, tR), (Kt, tK), (Vt, tV)]):
            if gi == 0 and rmode == "none":
                continue
            pair = zpair.next()
            if gi == 0 and rmode == "last":
                for kc in range(16):
                    add("pe", I("matmul", pb[pair[0]][:, 0:1], lhsT=WB[slots[0]][:, kc, :],
                                rhs=hT[:, kc, PT - 1:PT], start=(kc == 0), stop=(kc == 15)),
                        reads=["WB%d" % slots[0], "hT"], writes=[pbt(pair[0])])
                add("act", I("copy", out=shc[:, hp:hp + 1], in_=pb[pair[0]][:, 0:1]), reads=[pbt(pair[0]), "shc"],
                    writes=["shc"])
                yield
                continue
            yield from zmm_g(slots[gi], pair, samples, hT, "hT")
            yield from shift_block(gi * 8 + hp, pair, ps, dst, dtok)
        pD = pairr.next()
        pI = pairr.next()
        for (pr, lw_) in [(pD, LWD), (pI, LWI)]:
            for (b, pc, n, tc) in cols(pr, samples):
                add("pe", I("matmul", pb[b][:, pc:pc + n], lhsT=lora_w[:, hp * 128:(hp + 1) * 128],
                            rhs=lw_[:, tc:tc + n], start=True, stop=True),
                    reads=["lora_w0", "lora_w1", "LWt"], writes=[pbt(b)])
        for (b, pc, n, tc) in cols(pD, samples):
            add("act", I("activation", out=LOGD[:, tc:tc + n], in_=pb[b][:, pc:pc + n], func=AF.Sigmoid,
                         bias=pv(WD0, hp)), reads=[pbt(b), "pvec"], writes=[tLOGD])
        for (b, pc, n, tc) in cols(pI, samples):
            add("act", I("activation", out=AI[:, tc:tc + n], in_=pb[b][:, pc:pc + n], func=AF.Sigmoid,
                         bias=pv(WI0, hp)), reads=[pbt(b), "pvec"], writes=[tAI])
        yield
        add("act", I("activation", out=T1[:, 0:ncol], in_=Kt[:, 0:ncol], func=AF.Square, scale=pv(KK0, hp)),
            reads=[tK, "pvec"], writes=[tT1])
        yield
        pr = ones_mm(T1, tT1, samples)
        for (b, pc, n, tc) in cols(pr, samples):
            add("dve", I("tensor_scalar", out=T2[:, tc:tc + n], in0=pb[b][:, pc:pc + n], scalar1=1e-24,
                         scalar2=None, op0=ALU.max), reads=[pbt(b)], writes=[tT2])
        yield
        add("act", I("activation", out=T2[:, 0:ncol], in_=T2[:, 0:ncol], func=AF.Ln), reads=[tT2], writes=[tT2])
        add("act", I("activation", out=T2[:, 0:ncol], in_=T2[:, 0:ncol], func=AF.Exp, scale=-0.5), reads=[tT2],
            writes=[tT2])
        yield
        add("dve", I("scalar_tensor_tensor", out=AV[:, 0:ncol], in0=Kt[:, 0:ncol], scalar=nkk[:, hp:hp + 1],
                     in1=T2[:, 0:ncol], op0=ALU.mult, op1=ALU.mult), reads=[tK, tT2, "nkk"], writes=[tAV])
        yield
        add("dve", I("scalar_tensor_tensor", out=BV[:, 0:ncol], in0=AV[:, 0:ncol], scalar=-1.0,
                     in1=AI[:, 0:ncol], op0=ALU.mult, op1=ALU.mult), reads=[tAV, tAI], writes=[tBV])
        add("act", I("activation", out=T1[:, 0:ncol], in_=AI[:, 0:ncol], func=AF.Identity, scale=pv(KA0, hp),
                     bias=omka[:, hp:hp + 1]), reads=[tAI, "pvec", "omka"], writes=[tT1])
        yield
        add("dve", I("tensor_tensor", out=KM[:, 0:ncol], in0=Kt[:, 0:ncol], in1=T1[:, 0:ncol], op=ALU.mult),
            reads=[tK, tT1], writes=[tKM])
        yield
        if full:
            add("dve", I("scalar_tensor_tensor", out=T1[:, 0:ncol], in0=Rt[:, 0:ncol], scalar=pv(RK0, hp),
                         in1=KM[:, 0:ncol], op0=ALU.mult, op1=ALU.mult), reads=[tR, tKM, "pvec"], writes=[tT1])
            yield
            pr = ones_mm(T1, tT1, samples)
            for (b, pc, n, tc) in cols(pr, samples):
                add("dve", I("tensor_tensor", out=BON[:, tc:tc + n], in0=pb[b][:, pc:pc + n],
                             in1=Vt[:, tc:tc + n], op=ALU.mult), reads=[pbt(b), tV], writes=[tBON])
            yield
        add("dve", I("tensor_tensor_scan", out=CL[:, 0:PT], data0=segmask[:], data1=LOGD[:, 0:PT], initial=0.0,
                     op0=ALU.mult, op1=ALU.add), reads=["segmask", tLOGD], writes=[tCL])
        yield
        add("act", I("activation", out=EP[:, 0:PT], in_=CL[:, 0:PT], func=AF.Exp, scale=-DSC), reads=[tCL],
            writes=[tEP])
        add("act", I("activation", out=EM[:, 0:PT], in_=CL[:, 0:PT], func=AF.Exp, scale=DSC), reads=[tCL],
            writes=[tEM])
        yield
        add("dve", I("tensor_tensor", out=CL[:, 0:PT], in0=CL[:, 0:PT], in1=LOGD[:, 0:PT], op=ALU.subtract),
            reads=[tCL, tLOGD], writes=[tCL])
        add("act", I("activation", out=CL[:, 0:PT], in_=CL[:, 0:PT], func=AF.Exp, scale=-DSC), reads=[tCL],
            writes=[tCL])
        yield

    def c3(t):
        return t[:, 0:PT].rearrange("p (c l) -> p c l", l=64)

    def part2_build(hp, ps):
        samples, full = ps["samples"], ps["full"]
        par = hp % 2
        ARz, tAR = ARzP[par], "AR3p%d" % par
        DLt, tDL = DLtP[par], "DLt%d" % par
        EX, tEX = CL, tCL
        add("dve", I("tensor_tensor", out=KB3[:, :, 0:64], in0=c3(KM), in1=c3(EM), op=ALU.mult),
            reads=[tKM, tEM], writes=["KB3"])
        add("dve", I("tensor_tensor", out=KB3[:, :, 64:128], in0=c3(BV), in1=c3(EM), op=ALU.mult),
            reads=[tBV, tEM, "KB3"], writes=["KB3"])
        add("act", I("copy", out=VB[:], in_=Vt[:, 0:PT]), reads=[tV], writes=["VB"])
        for h2_ in range(2):
            r_ = rows[h2_]
            add("act", I("copy", out=KBz[h2_][r_, :, :], in_=KB3[r_, :, :]), reads=["KB3", "KBz"], writes=["KBz"])
        for h2_ in range(2):
            r_ = rows[h2_]
            add("dve", I("tensor_tensor", out=ARz[h2_][r_, :, 0:64], in0=c3(AV)[r_], in1=c3(EX)[r_], op=ALU.mult),
                reads=[tAV, tEX, tAR], writes=[tAR])
            if full:
                add("dve", I("tensor_tensor", out=ARz[h2_][r_, :, 64:128], in0=c3(Rt)[r_], in1=c3(EP)[r_],
                             op=ALU.mult), reads=[tR, tEP, tAR], writes=[tAR])
        add("act", I("copy", out=DLt[:], in_=EP[:, 0:PT].rearrange("p (c l) -> p c l", l=64)[:, :, 63]),
            reads=[tEP], writes=[tDL])
        if samples and full:
            SMP, tSMP = SMP2[:, hp % 3], "SMP%d" % (hp % 3)
            for i_, (t_, tk_) in enumerate([(AV, tAV), (BV, tBV), (KM, tKM), (Rt, tR), (Vt, tV)]):
                add("act", I("copy", out=SMP[:, i_, :], in_=t_[:, PT:PT + NS]), reads=[tk_, tSMP], writes=[tSMP])
            add("act", I("activation", out=SMP[:, 5, :], in_=LOGD[:, PT:PT + NS], func=AF.Exp, scale=-DSC),
                reads=[tLOGD, tSMP], writes=[tSMP])

    def part2(hp, ps):
        samples, full = ps["samples"], ps["full"]
        par = hp % 2
        ARz, tAR = ARzP[par], "AR3p%d" % par
        TOK, MSK, MSB = TOKP[par], MSKP[par], MSBP[par]
        tTOK = ["TOK%dp%d" % (i_, par) for i_ in range(3)]
        tMSK, tMSB = "MSKp%d" % par, "MSBp%d" % par
        nbq = nbr if full else nbr2
        for i3 in range(3):
            b = nbr.next()
            pv_ = pb[b][:].bitcast(BF16)
            for ch in range(NCH):
                if i3 == 0:
                    src = KB3[:, ch, 0:64]
                elif i3 == 1:
                    src = KB3[:, ch, 64:128]
                else:
                    src = VB[:, ch * 64:(ch + 1) * 64]
                add("pe", I("transpose", pv_[0:64, ch * 128:(ch + 1) * 128], src, ident_b[:]),
                    reads=["KB3" if i3 < 2 else "VB", "ident_b"], writes=[pbt(b)])
            add("act", I("copy", out=TOK[:, i3, :, :],
                         in_=pv_[0:64, 0:NCH * 128].rearrange("p (c m) -> p c m", m=128)),
                reads=[pbt(b)], writes=[tTOK[i3]])
            yield
        if full:
            for g in range(4):
                bK = nbr.next()
                bB = nbr.next()
                for q in range(4):
                    u = g * 4 + q
                    ch, h2 = u // 2, u % 2
                    add("pe", I("matmul", pb[bK][0:64, q * 128:(q + 1) * 128], lhsT=KBz[h2][:, ch, 0:64],
                                rhs=ARz[h2][:, ch, :], start=True, stop=True), reads=["KBz", tAR],
                        writes=[pbt(bK)])
                    add("pe", I("matmul", pb[bB][0:64, q * 128:(q + 1) * 128], lhsT=KBz[h2][:, ch, 64:128],
                                rhs=ARz[h2][:, ch, :], start=True, stop=True), reads=["KBz", tAR],
                        writes=[pbt(bB)])
                add("dve", I("tensor_tensor", out=MSK[:, g * 4:(g + 1) * 4, :],
                             in0=pb[bK][0:64, :].rearrange("p (a m) -> p a m", m=128), in1=mask4[:], op=ALU.mult),
                    reads=[pbt(bK), "mask4"], writes=[tMSK])
                add("dve", I("tensor_tensor", out=MSB[:, g * 4:(g + 1) * 4, :],
                             in0=pb[bB][0:64, :].rearrange("p (a m) -> p a m", m=128), in1=mask4[:], op=ALU.mult),
                    reads=[pbt(bB), "mask4"], writes=[tMSB])
                yield
        else:
            msk8 = mask4[:, 0:1, 0:64].broadcast_to([64, 8, 64])
            for g in range(2):
                bK = nbr.next()
                bB = nbr.next()
                for q in range(8):
                    u = g * 8 + q
                    ch, h2 = u // 2, u % 2
                    add("pe", I("matmul", pb[bK][0:64, q * 64:(q + 1) * 64], lhsT=KBz[h2][:, ch, 0:64],
                                rhs=ARz[h2][:, ch, 0:64], start=True, stop=True), reads=["KBz", tAR],
                        writes=[pbt(bK)])
                    add("pe", I("matmul", pb[bB][0:64, q * 64:(q + 1) * 64], lhsT=KBz[h2][:, ch, 64:128],
                                rhs=ARz[h2][:, ch, 0:64], start=True, stop=True), reads=["KBz", tAR],
                        writes=[pbt(bB)])
                add("dve", I("tensor_tensor", out=MSK[:, g * 8:(g + 1) * 8, 0:64],
                             in0=pb[bK][0:64, :].rearrange("p (a m) -> p a m", m=64), in1=msk8, op=ALU.mult),
                    reads=[pbt(bK), "mask4"], writes=[tMSK])
                add("dve", I("tensor_tensor", out=MSB[:, g * 8:(g + 1) * 8, 0:64],
                             in0=pb[bB][0:64, :].rearrange("p (a m) -> p a m", m=64), in1=msk8, op=ALU.mult),
                    reads=[pbt(bB), "mask4"], writes=[tMSB])
                yield
        for g in range(2):
            b = nbr.next()
            for q in range(8):
                u = g * 8 + q
                ch, h2 = u // 2, u % 2
                add("pe", I("matmul", pb[b][0:64, q * 64:(q + 1) * 64], lhsT=ARz[h2][:, ch, 0:64],
                            rhs=KBz[h2][:, ch, 64:128], start=True, stop=True), reads=["KBz", tAR],
                    writes=[pbt(b)])
            add("dve", I("tensor_tensor", out=Y1[:, g * 8:(g + 1) * 8, :],
                         in0=pb[b][0:64, :].rearrange("p (a m) -> p a m", m=64), in1=maskY[:], op=ALU.mult),
                reads=[pbt(b), "maskY"], writes=["YBtg%d" % g])
            yield
        add("dve", I("tensor_tensor", out=QA[:], in0=MSB[:, :, 0:64],
                     in1=I16[:].unsqueeze(1).broadcast_to([64, 16, 64]), op=ALU.add),
            reads=[tMSB, "I16"], writes=["QA"])

        def Xc(t, u):
            return MSB[:, u, 0:64] if t is None else t[:, u, :]
        curX, curXt, curY, curYt = None, [tMSB], Y1, ["YBtg0", "YBtg1"]
        nxt = [(XA, "XA", YA, "YA"), (XBt, "XBt", YBt, "YBt")]
        Q, Qt, Qn, Qnt = QA, "QA", QB, "QB"
        pend = None

        def q_update(nY_, nYt_, Q_, Qt_, Qn_, Qnt_):
            for g in range(2):
                bQ = nbq.next()
                for q in range(8):
                    u = g * 8 + q
                    add("pe", I("matmul", pb[bQ][0:64, q * 64:(q + 1) * 64], lhsT=nY_[:, u, :], rhs=Q_[:, u, :],
                                start=True, stop=True), reads=nYt_ + [Qt_], writes=[pbt(bQ)])
                add("dve", I("tensor_tensor", out=Qn_[:, g * 8:(g + 1) * 8, :],
                             in0=pb[bQ][0:64, :].rearrange("p (a m) -> p a m", m=64),
                             in1=Q_[:, g * 8:(g + 1) * 8, :], op=ALU.add), reads=[pbt(bQ), Qt_], writes=[Qnt_])
        for lev in range(5):
            nX, nXt, nY, nYt = nxt[lev % 2]
            last = lev == 4
            for g in range(2):
                bX = nbq.next() if not last else None
                bY = nbq.next()
                for q in range(8):
                    u = g * 8 + q
                    if not last:
                        add("pe", I("matmul", pb[bX][0:64, q * 64:(q + 1) * 64], lhsT=curY[:, u, :],
                                    rhs=Xc(curX, u), start=True, stop=True), reads=curXt + curYt,
                            writes=[pbt(bX)])
                    add("pe", I("matmul", pb[bY][0:64, q * 64:(q + 1) * 64], lhsT=Xc(curX, u),
                                rhs=curY[:, u, :], start=True, stop=True), reads=curXt + curYt, writes=[pbt(bY)])
                ev = "act" if g == 0 else "dve"
                cp = "copy" if g == 0 else "tensor_copy"
                add(ev, I(cp, out=nY[:, g * 8:(g + 1) * 8, :],
                          in_=pb[bY][0:64, :].rearrange("p (a m) -> p a m", m=64)),
                    reads=[pbt(bY)], writes=[nYt + "g%d" % g])
                if not last:
                    add(ev, I(cp, out=nX[:, g * 8:(g + 1) * 8, :],
                              in_=pb[bX][0:64, :].rearrange("p (a m) -> p a m", m=64)),
                        reads=[pbt(bX)], writes=[nXt + "g%d" % g])
            yield
            if pend is not None:
                q_update(*pend)
                yield
            pend = (nY, [nYt + "g0", nYt + "g1"], Q, Qt, Qn, Qnt)
            curX, curXt, curY, curYt = nX, [nXt + "g0", nXt + "g1"], nY, [nYt + "g0", nYt + "g1"]
            Q, Qt, Qn, Qnt = Qn, Qnt, Q, Qt
        q_update(*pend)
        yield
        add("act", I("copy", out=TTbP[par][:], in_=Q[:]), reads=[Qt], writes=["TTp%d" % par])
        yield

    BY = 4

    def chain(hp, ps):
        samples, full = ps["samples"], ps["full"]
        par = hp % 2
        ARz, tAR = ARzP[par], "AR3p%d" % par
        TOK, MSK, MSB = TOKP[par], MSKP[par], MSBP[par]
        tTOK = ["TOK%dp%d" % (i_, par) for i_ in range(3)]
        tMSK, tMSB = "MSKp%d" % par, "MSBp%d" % par
        TT, TTt = TTbP[par], "TTp%d" % par
        DLt, tDL = DLtP[par], "DLt%d" % par
        add("act", I("copy", out=Hball[:, 0, :], in_=Hf[:, hp, :]), reads=["Hf"], writes=["Hball"])
        for ch in range(NCH):
            bW = nbr.next()
            for h2 in range(2):
                u = ch * 2 + h2
                cs = slice(h2 * 64, (h2 + 1) * 64)
                add("pe", I("matmul", pb[bW][0:64, cs], lhsT=MSK[:, u, 0:64], rhs=TOK[:, 2, ch, cs], start=True,
                            stop=False), reads=[tMSK, tTOK[2]], writes=[pbt(bW)])
                add("pe", I("matmul", pb[bW][0:64, cs], lhsT=ARz[h2][:, ch, 0:64], rhs=Hball[:, ch, :],
                            start=False, stop=True), reads=[tAR, "Hball"], writes=[pbt(bW)])
            add("act", I("copy", out=WS[:], in_=pb[bW][0:64, 0:128]), reads=[pbt(bW)], writes=["WS"])
            yield
            bU = nbr.next()
            for h2 in range(2):
                u = ch * 2 + h2
                cs = slice(h2 * 64, (h2 + 1) * 64)
                add("pe", I("matmul", pb[bU][0:64, cs], lhsT=TT[:, u, :], rhs=WS[:, cs], start=True, stop=True),
                    reads=[TTt, "WS"], writes=[pbt(bU)])
            add("dve", I("tensor_copy", out=US[:, ch, :], in_=pb[bU][0:64, 0:128]), reads=[pbt(bU)],
                writes=["US"])
            yield
            bH = nbr.next()
            for h2 in range(2):
                rs_ = rows[h2]
                cs = slice(h2 * 64, (h2 + 1) * 64)
                add("pe", I("matmul", pb[bH][rs_, 0:64], lhsT=TOK[:, 0, ch, cs], rhs=TOK[:, 2, ch, cs],
                            start=True, stop=False), reads=[tTOK[0], tTOK[2]], writes=[pbt(bH)])
            for h2 in range(2):
                rs_ = rows[h2]
                cs = slice(h2 * 64, (h2 + 1) * 64)
                add("pe", I("matmul", pb[bH][rs_, 0:64], lhsT=TOK[:, 1, ch, cs], rhs=US[:, ch, cs],
                            start=False, stop=True), reads=[tTOK[1], "US"], writes=[pbt(bH)])
            dl = DLt[:, ch:ch + 1]
            add("act", I("activation", out=HDL[:], in_=Hf[:, hp, :], func=AF.Identity, scale=dl),
                reads=["Hf", tDL], writes=["HDL"])
            add("dve", I("scalar_tensor_tensor", out=Hball[:, ch + 1, :], in0=pb[bH][:, 0:64], scalar=dl, in1=HDL[:],
                         op0=ALU.mult, op1=ALU.add), reads=[pbt(bH), "HDL", tDL], writes=["Hball"])
            add("dve", I("scalar_tensor_tensor", out=Hf[:, hp, :], in0=pb[bH][:, 0:64], scalar=dl, in1=HDL[:],
                         op0=ALU.mult, op1=ALU.add), reads=[pbt(bH), "HDL", tDL], writes=["Hf"])
            yield
            if full:
                for h2 in range(2):
                    u = ch * 2 + h2
                    rs_ = rows[h2]
                    cs = slice(h2 * 64, (h2 + 1) * 64)
                    tcs = slice(ch * 64, (ch + 1) * 64)
                    add("pe", I("matmul", pb[BY][rs_, tcs], lhsT=Hball[:, ch, :], rhs=ARz[h2][:, ch, 64:128],
                                start=True, stop=False), reads=["Hball", tAR], writes=[pbt(BY)])
                    add("pe", I("matmul", pb[BY][rs_, tcs], lhsT=US[:, ch, cs], rhs=MSB[:, u, 64:128],
                                start=False, stop=False), reads=["US", tMSB], writes=[pbt(BY)])
                    add("pe", I("matmul", pb[BY][rs_, tcs], lhsT=TOK[:, 2, ch, cs], rhs=MSK[:, u, 64:128],
                                start=False, stop=True), reads=[tTOK[2], tMSK], writes=[pbt(BY)])
                yield

    def post(hp, ps):
        samples, full = ps["samples"], ps["full"]
        if not full:
            return
        ncol = PT + (NS if samples else 0)
        BON, tBON = BONs[hp % 4], tBONs[hp % 4]
        slot_rg = zblock(3200 + hp * 128, samples)
        yield
        if samples:
            SMP, tSMP = SMP2[:, hp % 3], "SMP%d" % (hp % 3)
            T3 = WA[:, 13 * NCOLMAX:13 * NCOLMAX + NS * 64]
            T3v = T3.rearrange("p (s k) -> p s k", k=64)
            DG = WA[:, 20 * NCOLMAX:20 * NCOLMAX + NS * 32].bitcast(BF16)
            DGv = DG.rearrange("p (s k) -> p s k", k=64)
            tT3, tDG = [wt[13], wt[14]], [wt[20], wt[21]]
            src = swkv_d.rearrange("s (hp h2) v k -> hp (h2 v) s k", h2=2)[hp]
            add("sp", I("dma_start", out=SWt[:], in_=src), writes=["SWt"], dma="SWt")

            def bcast(i_):
                add("dve", I("tensor_tensor", out=DGv, in0=E2[:].unsqueeze(1).broadcast_to([128, NS, 64]),
                             in1=SMP[:, i_, :].unsqueeze(2).broadcast_to([128, NS, 64]), op=ALU.mult),
                    reads=["E2", tSMP], writes=tDG)
                pr = (5, 7)
                for hf in range(2):
                    add("pe", I("matmul", pb[pr[hf]][:, 0:512], lhsT=ones_b[:], rhs=DG[:, hf * 512:(hf + 1) * 512],
                                start=True, stop=True), reads=["ones_b"] + tDG, writes=[pbt(pr[hf])])
                return pr

            def halves(pr):
                for hf in range(2):
                    yield (pb[pr[hf]][:, 0:512].rearrange("p (s k) -> p s k", k=64), pbt(pr[hf]),
                           slice(hf * 8, (hf + 1) * 8))
            pr = bcast(0)
            for (pv3, ptok, ss_) in halves(pr):
                add("dve", I("tensor_tensor", out=T3v[:, ss_, :], in0=SWt[:, ss_, :], in1=pv3, op=ALU.mult),
                    reads=["SWt", ptok], writes=tT3)
            add("dve", I("tensor_reduce", out=SA[:], in_=T3v, axis=AX.X, op=ALU.add), reads=tT3, writes=["SA"])
            yield
            pr = bcast(5)
            for (pv3, ptok, ss_) in halves(pr):
                add("dve", I("tensor_tensor", out=SWt[:, ss_, :], in0=SWt[:, ss_, :], in1=pv3, op=ALU.mult),
                    reads=["SWt", ptok], writes=["SWt"])
            yield
            pr = bcast(1)
            for (pv3, ptok, ss_) in halves(pr):
                add("dve", I("tensor_tensor", out=T3v[:, ss_, :], in0=pv3,
                             in1=SA[:, ss_].unsqueeze(2).broadcast_to([128, 8, 64]), op=ALU.mult),
                    reads=["SA", ptok], writes=tT3)
            add("dve", I("tensor_tensor", out=SWt[:], in0=SWt[:], in1=T3v, op=ALU.add), reads=["SWt"] + tT3,
                writes=["SWt"])
            yield
            pr = bcast(2)
            for (pv3, ptok, ss_) in halves(pr):
                add("dve", I("tensor_tensor", out=T3v[:, ss_, :], in0=pv3,
                             in1=SMP[:, 4, ss_].unsqueeze(2).broadcast_to([128, 8, 64]),
                             op=ALU.mult), reads=[tSMP, ptok], writes=tT3)
            add("dve", I("tensor_tensor", out=SWt[:], in0=SWt[:], in1=T3v, op=ALU.add), reads=["SWt"] + tT3,
                writes=["SWt"])
            yield
            pr = bcast(3)
            for (pv3, ptok, ss_) in halves(pr):
                add("dve", I("tensor_tensor", out=T3v[:, ss_, :], in0=SWt[:, ss_, :], in1=pv3, op=ALU.mult),
                    reads=["SWt", ptok], writes=tT3)
            add("dve", I("tensor_reduce", out=YT[:, PT:PT + NS], in_=T3v, axis=AX.X, op=ALU.add), reads=tT3,
                writes=[tYT])
            dst = o_wkvs.rearrange("s (hp h2) v k -> hp (h2 v) s k", h2=2)[hp]
            add("sp", I("dma_start", out=dst, in_=SWt[:]), reads=["SWt"], writes=["o_wkvs"], dma="o_wkvs")
            yield
        pr = ones_mm(YT, tYT, samples)
        for (b, pc, n, tc) in cols(pr, samples):
            add("act", I("activation", out=T2p[:, tc:tc + n], in_=pb[b][:, pc:pc + n], func=AF.Identity,
                         scale=1.0 / 64), reads=[pbt(b)], writes=[tT2p])
        add("act", I("activation", out=T1p[:, 0:ncol], in_=YT[:, 0:ncol], func=AF.Square), reads=[tYT],
            writes=[tT1p])
        yield
        add("dve", I("tensor_tensor", out=YT[:, 0:ncol], in0=YT[:, 0:ncol], in1=T2p[:, 0:ncol], op=ALU.subtract),
            reads=[tYT, tT2p], writes=[tYT])
        add("act", I("activation", out=T2p[:, 0:ncol], in_=T2p[:, 0:ncol], func=AF.Square), reads=[tT2p],
            writes=[tT2p])
        yield
        pr = ones_mm(T1p, tT1p, samples)
        for (b, pc, n, tc) in cols(pr, samples):
            add("dve", I("scalar_tensor_tensor", out=T1p[:, tc:tc + n], in0=pb[b][:, pc:pc + n], scalar=1.0 / 64,
                         in1=T2p[:, tc:tc + n], op0=ALU.mult, op1=ALU.subtract), reads=[pbt(b), tT2p],
                writes=[tT1p])
        yield
        add("act", I("activation", out=T1p[:, 0:ncol], in_=T1p[:, 0:ncol], func=AF.Ln, bias=epsc[:, 0:1]),
            reads=[tT1p, "epsc"], writes=[tT1p])
        add("act", I("activation", out=T1p[:, 0:ncol], in_=T1p[:, 0:ncol], func=AF.Exp, scale=-0.5), reads=[tT1p],
            writes=[tT1p])
        yield
        add("dve", I("tensor_tensor", out=YT[:, 0:ncol], in0=YT[:, 0:ncol], in1=T1p[:, 0:ncol], op=ALU.mult),
            reads=[tYT, tT1p], writes=[tYT])
        add("act", I("activation", out=YT[:, 0:ncol], in_=YT[:, 0:ncol], func=AF.Identity, scale=pv(LG0, hp),
                     bias=pv(LB0, hp)), reads=[tYT, "pvec"], writes=[tYT])
        yield
        add("dve", I("tensor_tensor", out=YT[:, 0:ncol], in0=YT[:, 0:ncol], in1=BON[:, 0:ncol], op=ALU.add),
            reads=[tYT, tBON], writes=[tYT])
        pair = pairr.next()
        for _ in zmm_g(slot_rg, pair, samples, hT, "hT"):
            pass
        for (b, pc, n, tc) in cols(pair, samples):
            add("act", I("activation", out=SILp[:, tc:tc + n], in_=pb[b][:, pc:pc + n], func=AF.Sigmoid),
                reads=[pbt(b)], writes=[tSILp])
            add("dve", I("tensor_tensor", out=SILp[:, tc:tc + n], in0=SILp[:, tc:tc + n], in1=pb[b][:, pc:pc + n],
                         op=ALU.mult), reads=[pbt(b), tSILp], writes=[tSILp])
        yield
        add("dve", I("tensor_tensor", out=ORT[:, hp, 0:ncol], in0=YT[:, 0:ncol], in1=SILp[:, 0:ncol], op=ALU.mult),
            reads=[tYT, tSILp], writes=["ORT"])
        yield

    def run_g(g):
        if g is None:
            return
        for _ in g:
            pass

    def interleave(*gr):
        alive = [[g, 0.0] for (g, r) in gr if g is not None]
        while alive:
            item = min(alive, key=lambda it: it[1])
            n0 = len(P.fin)
            try:
                next(item[0])
            except StopIteration:
                alive.remove(item)
                continue
            if len(P.fin) > n0:
                item[1] = max(P.fin[n0:])
            else:
                item[1] += 0.05

    def lru_block(lb, ps, ts_=0):
        samples, full = ps["samples"], ps["full"]
        ncol = PT + (NS if samples else 0)
        idx_ = [0, 1, 3, 4, 5, 6, 7, 8, 9, 10] if ts_ == 0 else [11, 12, 13, 14, 15, 16, 17, 18, 19, 20]
        XP, XC, GX, GA, A_, E2t, MU, SILg, HS, XS = [W[i_] for i_ in idx_]
        tXP, tXC, tGX, tGA, tA, tE2t, tMU, tSIL, tHS, tXS = [wt[i_] for i_ in idx_]
        XCB, tXCB = (XCB0, "XCB") if ts_ == 0 else (KB3[:].rearrange("p a b -> p (a b)")[:, 0:NCOLMAX], "KB3")
        slot = zblock(4224 + lb * 128, samples)
        slot2 = zblock(5248 + lb * 128, samples) if full else None
        pair = zpair.next()
        zmm(slot, pair, samples, hT, "hT")
        add("act", I("copy", out=XP[:, 3:3 + PT], in_=pb[pair[0]][:, 0:PT]), reads=[pbt(pair[0])], writes=[tXP])
        add("dve", I("tensor_copy", out=XP[:, 0:3], in_=convc[:, :, lb]), reads=["convc"], writes=[tXP])
        if samples:
            add("act", I("copy", out=XS[:, 0:NS], in_=pb[pair[1]][:, 0:NS]), reads=[pbt(pair[1])], writes=[tXS])
            add("act", I("copy", out=CONVS[:, lb, :], in_=XS[:, 0:NS]), reads=[tXS], writes=["CONVS"])
        add("dve", I("tensor_scalar", out=XC[:, 0:PT], in0=XP[:, 3:3 + PT], scalar1=pv(CW0, 3 * 8 + lb),
                     scalar2=pv(CB0, lb), op0=ALU.mult, op1=ALU.add), reads=[tXP, "pvec"], writes=[tXC])
        for j in range(3):
            add("dve", I("scalar_tensor_tensor", out=XC[:, 0:PT], in0=XP[:, j:j + PT], scalar=pv(CW0, j * 8 + lb),
                         in1=XC[:, 0:PT], op0=ALU.mult, op1=ALU.add), reads=[tXP, tXC, "pvec"], writes=[tXC])
        add("dve", I("tensor_copy", out=convc[:, :, lb], in_=XP[:, PT:PT + 3]), reads=[tXP], writes=["convc"])
        yield
        if samples:
            add("dve", I("tensor_scalar", out=XC[:, PT:PT + NS], in0=XS[:, 0:NS], scalar1=pv(CW0, 3 * 8 + lb),
                         scalar2=pv(CB0, lb), op0=ALU.mult, op1=ALU.add), reads=[tXS, "pvec"], writes=[tXC])
            for j in range(3):
                add("dve", I("scalar_tensor_tensor", out=XC[:, PT:PT + NS], in0=convT[:, lb, j, :],
                             scalar=pv(CW0, j * 8 + lb), in1=XC[:, PT:PT + NS], op0=ALU.mult, op1=ALU.add),
                    reads=["convT", tXC, "pvec"], writes=[tXC])
        add("act", I("copy", out=XCB[:, 0:ncol], in_=XC[:, 0:ncol]), reads=[tXC], writes=[tXCB])
        yield
        pX = pairr.next()
        pA = pairr.next()
        for (pr, wt_, wtok) in [(pX, GXW, "GXW"), (pA, GAW, "GAW")]:
            for (b, pc, n, tc) in cols(pr, samples):
                add("pe", I("matmul", pb[b][:, pc:pc + n], lhsT=wt_[:, lb, :], rhs=XCB[:, tc:tc + n], start=True,
                            stop=True), reads=[wtok, tXCB], writes=[pbt(b)])
        for (b, pc, n, tc) in cols(pX, samples):
            add("act", I("activation", out=GX[:, tc:tc + n], in_=pb[b][:, pc:pc + n], func=AF.Sigmoid,
                         bias=pv(GXB0, lb)), reads=[pbt(b), "pvec"], writes=[tGX])
        for (b, pc, n, tc) in cols(pA, samples):
            add("act", I("activation", out=GA[:, tc:tc + n], in_=pb[b][:, pc:pc + n], func=AF.Sigmoid,
                         bias=pv(GAB0, lb)), reads=[pbt(b), "pvec"], writes=[tGA])
        yield
        add("dve", I("tensor_tensor", out=GX[:, 0:ncol], in0=GX[:, 0:ncol], in1=XC[:, 0:ncol], op=ALU.mult),
            reads=[tGX, tXC], writes=[tGX])
        add("act", I("activation", out=A_[:, 0:ncol], in_=GA[:, 0:ncol], func=AF.Exp, scale=nsp[:, lb:lb + 1]),
            reads=[tGA, "nsp"], writes=[tA])
        add("act", I("activation", out=E2t[:, 0:ncol], in_=GA[:, 0:ncol], func=AF.Exp, scale=nsp2[:, lb:lb + 1]),
            reads=[tGA, "nsp2"], writes=[tE2t])
        add("act", I("activation", out=E2t[:, 0:ncol], in_=E2t[:, 0:ncol], func=AF.Ln, scale=-1.0,
                     bias=ones_c[:, 0:1]), reads=[tE2t, "ones_c"], writes=[tE2t])
        add("act", I("activation", out=MU[:, 0:ncol], in_=E2t[:, 0:ncol], func=AF.Exp, scale=0.5), reads=[tE2t],
            writes=[tMU])
        yield
        add("dve", I("tensor_tensor", out=MU[:, 0:ncol], in0=MU[:, 0:ncol], in1=GX[:, 0:ncol], op=ALU.mult),
            reads=[tMU, tGX], writes=[tMU])
        add("dve", I("tensor_tensor_scan", out=HS[:, 0:PT], data0=A_[:, 0:PT], data1=MU[:, 0:PT],
                     initial=hcar[:, lb:lb + 1], op0=ALU.mult, op1=ALU.add), reads=[tA, tMU, "hcar"],
            writes=[tHS])
        add("dve", I("tensor_copy", out=hcar[:, lb:lb + 1], in_=HS[:, PT - 1:PT]), reads=[tHS], writes=["hcar"])
        yield
        if samples:
            add("dve", I("tensor_tensor", out=HS[:, PT:PT + NS], in0=A_[:, PT:PT + NS], in1=hlruT[:, lb, :],
                         op=ALU.mult), reads=[tA, "hlruT", tHS], writes=[tHS])
            add("dve", I("tensor_tensor", out=HS[:, PT:PT + NS], in0=HS[:, PT:PT + NS], in1=MU[:, PT:PT + NS],
                         op=ALU.add), reads=[tMU, tHS], writes=[tHS])
            add("act", I("copy", out=LRUS[:, lb, :], in_=HS[:, PT:PT + NS]), reads=[tHS], writes=["LRUS"])
        if full:
            pair = zpair.next()
            zmm(slot2, pair, samples, hT, "hT")
            for (b, pc, n, tc) in cols(pair, samples):
                add("act", I("activation", out=SILg[:, tc:tc + n], in_=pb[b][:, pc:pc + n], func=AF.Sigmoid),
                    reads=[pbt(b)], writes=[tSIL])
                add("dve", I("tensor_tensor", out=SILg[:, tc:tc + n], in0=SILg[:, tc:tc + n],
                             in1=pb[b][:, pc:pc + n], op=ALU.mult), reads=[pbt(b), tSIL], writes=[tSIL])
            add("dve", I("tensor_tensor", out=OGT[:, lb, 0:ncol], in0=HS[:, 0:ncol], in1=SILg[:, 0:ncol],
                         op=ALU.mult), reads=[tHS, tSIL], writes=["OGT"])
        yield

    def merge_block(j, ps):
        samples = ps["samples"]
        ncol = PT + (NS if samples else 0)
        SG, M1, M2 = W[13], W[14], W[15]
        p_ = j % 2
        s_r = (ARzP[p_][0], "ARzM%d0" % p_)
        s_g = (ARzP[p_][1], "ARzM%d1" % p_)
        for (st_, wsrc_) in ((s_r, worw_d), (s_g, wolru_d)):
            add("pool", I("dma_start", out=st_[0][:, 0:8, :], in_=wsrc_[j]),
                writes=["AR3p%d" % p_, st_[1]], dma="ab_" + st_[1])
        s_mr = zblock(6272 + j * 128, samples)
        s_mg = zblock(8320 + j * 128, samples)
        for (s_y, s_m, src, stok, Mx, mtok) in [(s_r, s_mr, ORT, "ORT", M1, wt[14]),
                                                (s_g, s_mg, OGT, "OGT", M2, wt[15])]:
            pY = zpair.next()
            zmm(s_y, pY, samples, src, stok, kchunks=8)
            pM = pairr.next()
            zmm(s_m, pM, samples, hT, "hT")
            for (b, pc, n, tc) in cols(pM, samples):
                add("act", I("activation", out=SG[:, tc:tc + n], in_=pb[b][:, pc:pc + n], func=AF.Sigmoid),
                    reads=[pbt(b)], writes=[wt[13]])
            for (b, pc, n, tc) in cols(pY, samples):
                add("dve", I("tensor_tensor", out=Mx[:, tc:tc + n], in0=pb[b][:, pc:pc + n], in1=SG[:, tc:tc + n],
                             op=ALU.mult), reads=[pbt(b), wt[13]], writes=[mtok])
        add("dve", I("tensor_tensor", out=MT[:, j, 0:ncol], in0=M1[:, 0:ncol], in1=M2[:, 0:ncol], op=ALU.add),
            reads=[wt[14], wt[15]], writes=["ARENA1"])

    def final_stage(ps):
        samples = ps["samples"]
        src = xsrc[ps["src"]]
        row0 = ps["row0"]
        tiles_all = [(tt, 128) for tt in range(4)] + ([(4, NS)] if samples else [])
        FG, fgt = OGT[:].rearrange("p a b -> p (a b)").bitcast(F32)[:, 0:D], ["OGT"]
        OUT4 = ORT[:].rearrange("p a b -> p (a b)").bitcast(F32)[:, 0:D]
        JUNK = AR1[:, 0:D]
        add("sp", I("dma_start", out=FG, in_=fng_d.partition_broadcast(128)), writes=fgt, dma="FG")
        OUTB, otk = {}, {}
        for (t, m) in tiles_all:
            if t < 4:
                OUTB[t] = WA[:, t * D:(t + 1) * D]
                otk[t] = wtoks(t * D, (t + 1) * D)
                add("sp", I("dma_start", out=OUTB[t][:, :], in_=src[row0 + t * 128:row0 + (t + 1) * 128, :]),
                    writes=otk[t], dma="OUTB%d" % t)
            else:
                OUTB[t] = OUT4
                otk[t] = ["ORT"]
                add("sp", I("dma_start", out=OUTB[t][0:NS, :], in_=xs_d[:, :]), writes=otk[t], dma="OUTB%d" % t)
        for cg in range(4):
            for q4 in range(4):
                slot = wrot.next()
                wv = WB[slot][:].rearrange("p a b -> p (a b)").rearrange("p (q c) -> p q c", c=512)
                add("pool", I("dma_start", out=wv, in_=wout_d[cg, q4]),
                    writes=["WB%d" % slot], dma="WB%d" % slot)
                for (t, m) in tiles_all:
                    c0 = t * 128
                    for jj in range(4):
                        j = q4 * 4 + jj
                        add("pe", I("matmul", pb[t][0:m, 0:512], lhsT=MT[:, j, c0:c0 + m], rhs=wv[:, jj, :],
                                    start=(j == 0), stop=(j == 15)), reads=["ARENA1", "WB%d" % slot],
                            writes=[pbt(t)])
            for (t, m) in tiles_all:
                add("dve", I("tensor_tensor", out=OUTB[t][0:m, cg * 512:(cg + 1) * 512], in0=pb[t][0:m, 0:512],
                             in1=OUTB[t][0:m, cg * 512:(cg + 1) * 512], op=ALU.add),
                    reads=[pbt(t)] + otk[t], writes=otk[t])
        for (t, m) in tiles_all:
            add("act", I("activation", out=JUNK[0:m, :], in_=OUTB[t][0:m, :], func=AF.Square,
                         accum_out=ssq[0:m, t:t + 1]), reads=otk[t], writes=["ARENA1", "ssq"])
            add("act", I("activation", out=tq2[0:m, t:t + 1], in_=ssq[0:m, t:t + 1], func=AF.Ln,
                         scale=1.0 / D, bias=epsc[0:m, 1:2]), reads=["ssq", "epsc"], writes=["tq2"])
            add("act", I("activation", out=rstd[0:m, t:t + 1], in_=tq2[0:m, t:t + 1], func=AF.Exp, scale=-0.5),
                reads=["tq2"], writes=["rstd"])
            add("dve", I("scalar_tensor_tensor", out=OUTB[t][0:m, :], in0=OUTB[t][0:m, :],
                         scalar=rstd[0:m, t:t + 1], in1=FG[0:m, :], op0=ALU.mult, op1=ALU.mult),
                reads=otk[t] + ["rstd"] + fgt, writes=otk[t])
            if t < 4:
                r0 = ps["orow0"] + t * 128
                add("sp", I("dma_start", out=yp_d[r0:r0 + 128, :], in_=OUTB[t][:, :]), reads=otk[t],
                    writes=["o_yp%d_%d" % (ps["idx"], t)], dma="o_yp%d" % t)
                out_tokens.append("o_yp%d_%d" % (ps["idx"], t))
            else:
                add("sp", I("dma_start", out=ys_d[:, :], in_=OUTB[t][0:NS, :]), reads=otk[t], writes=["o_ys"],
                    dma="o_ys")
                out_tokens.append("o_ys")

    for ps in passes:
        samples, full = ps["samples"], ps["full"]
        ncol = PT + (NS if samples else 0)
        if ps.get("mask_before"):
            for (t_, tok) in [(Hf[:].rearrange("p a b -> p (a b)"), "Hf"), (shc[:], "shc"),
                              (convc[:].rearrange("p a b -> p (a b)"), "convc"), (hcar[:], "hcar")]:
                add("dve", I("tensor_scalar", out=t_, in0=t_, scalar1=flag[:, 0:1], scalar2=None, op0=ALU.mult),
                    reads=[tok, "flag"], writes=[tok])
        for p_ in range(2):
            for h2_ in range(2):
                o_ = slice((1 - h2_) * 64, (2 - h2_) * 64)
                add("dve", I("memset", ARzP[p_][h2_][o_, :, :], 0.0),
                    reads=[], writes=["AR3p%d" % p_, "ARzM%d%d" % (p_, h2_)])
        MARKS.append((len(P.ops), "pass%d start" % ps["idx"]))
        stage_x(ps)
        MARKS.append((len(P.ops), "after stage_x"))
        s24 = zblock(3072, samples)
        pair = zpair.next()
        zmm(s24, pair, samples, hT, "hT")
        run_g(shift_block(24, pair, ps, W[2], wt[2]))
        add("act", I("activation", out=LWD[0:64, 0:ncol], in_=W[2][0:64, 0:ncol], func=AF.Tanh),
            reads=[wt[2], "LWt"], writes=["LWt"])
        add("act", I("copy", out=LWI[64:128, 0:ncol], in_=W[2][64:128, 0:ncol]), reads=[wt[2], "LWt"],
            writes=["LWt"])
        run_g(part1(0, ps))
        part2_build(0, ps)
        interleave((part2(0, ps), 1), (part1(1, ps), 1))
        for hp in range(8):
            MARKS.append((len(P.ops), "wkv hp%d" % hp))
            if hp + 1 < 8:
                part2_build(hp + 1, ps)
            interleave((chain(hp, ps), 1),
                       (part2(hp + 1, ps) if hp + 1 < 8 else None, 2),
                       (part1(hp + 2, ps) if hp + 2 < 8 else None, 2),
                       (post(hp - 1, ps) if hp >= 1 else None, 1))
            if full:
                add("act", I("copy", out=YT[:, 0:PT], in_=pb[BY][:, 0:PT]), reads=[pbt(BY)], writes=[tYT])
        run_g(post(7, ps))
        for lb in range(0, 8, 2):
            MARKS.append((len(P.ops), "lru lb%d" % lb))
            interleave((lru_block(lb, ps, 0), 1), (lru_block(lb + 1, ps, 1), 1))
        MARKS.append((len(P.ops), "after lru"))
        if full:
            for j in range(16):
                merge_block(j, ps)
            final_stage(ps)

    def out_T(src_ap, n, dst_ap, tok, src_tok):
        b = nbr.next()
        add("pe", I("transpose", pb[b][0:n, 0:128], src_ap, ident_f[:]), reads=[src_tok, "ident_f"],
            writes=[pbt(b)])
        add("dve", I("tensor_copy", out=SMALLO[0:n, :], in_=pb[b][0:n, 0:128]), reads=[pbt(b)], writes=[wt[0]])
        add("sp", I("dma_start", out=dst_ap, in_=SMALLO[0:n, :]), reads=[wt[0]], writes=[tok], dma=tok)
        out_tokens.append(tok)

    out_T(shc[:], 25, o_shp[:, :], "o_shp", "shc")
    out_T(hcar[:], 8, o_lrup[:, :], "o_lrup", "hcar")
    for j in range(3):
        out_T(convc[:, j, :], 8, o_convp[j], "o_convp%d" % j, "convc")
    for hp in range(8):
        b = nbr.next()
        add("pe", I("transpose", pb[b][0:64, 0:128], Hf[:, hp, :], ident_f[:]), reads=["Hf", "ident_f"],
            writes=[pbt(b)])
        add("dve", I("tensor_copy", out=WKVO[:, hp, :], in_=pb[b][0:64, 0:128]), reads=[pbt(b)], writes=[wt[0], wt[1]])
    for hp in range(8):
        add("sp", I("dma_start", out=o_wkvp[2 * hp:2 * hp + 2].rearrange("h v k -> v h k"),
                    in_=WKVO[:, hp, :].rearrange("p (h k) -> p h k", k=64)), reads=[wt[0], wt[1]],
            writes=["o_wkvp%d" % hp], dma="o_wkvp")
        out_tokens.append("o_wkvp%d" % hp)
    out_tokens.append("o_wkvs")

    def out_rows(srcT, nblk, dst_ap, tok, src_tok):
        stg2 = WA[0:NS, 0:nblk * 128]
        toks = [wt[i] for i in range(0, (nblk * 128 - 1) // NCOLMAX + 1)]
        for g0 in range(0, nblk, 4):
            b = nbr.next()
            n = min(4, nblk - g0)
            for q in range(n):
                add("pe", I("transpose", pb[b][0:NS, q * 128:(q + 1) * 128], srcT[:, g0 + q, :], ident_f[:]),
                    reads=[src_tok, "ident_f"], writes=[pbt(b)])
            add("dve", I("tensor_copy", out=stg2[:, g0 * 128:(g0 + n) * 128], in_=pb[b][0:NS, 0:n * 128]),
                reads=[pbt(b)], writes=toks)
        add("sp", I("dma_start", out=dst_ap, in_=stg2), reads=toks, writes=[tok], dma=tok)
        out_tokens.append(tok)

    out_rows(shiftOutT, 25, o_shs[:, :], "o_shs", "shiftOutT")
    out_rows(LRUS, 8, o_lrus[:, :], "o_lrus", "LRUS")
    out_rows(CONVS, 8, o_convs[:, 2, :], "o_convs2", "CONVS")
    add("sp", I("dma_start", out=o_convs[:, 0:2, :], in_=sconv_d[:, 1:3, :]), writes=["o_convs01"], dma="o_convs01")
    out_tokens.append("o_convs01")
    add("sp", None, reads=list(dict.fromkeys(out_tokens)))
    stats = P.emit()
    return nc, stats


PASSES = [
    dict(idx=0, src="xprev", row0=0, full=False, samples=False, rmode="none"),
    dict(idx=1, src="xprev", row0=512, full=False, samples=False, rmode="last"),
    dict(idx=2, src="xown", row0=0, orow0=0, full=True, samples=False, mask_before=True),
    dict(idx=3, src="xown", row0=512, orow0=512, full=True, samples=True),
]

_CACHE = {}


def _pvec(inp):
    def colz(v):
        v = np.asarray(v, np.float32).reshape(-1)
        return v.reshape(-1, 128).T
    parts = [colz(inp["norm_g"][0]), colz(inp["rwkv_mu"][0]), colz(inp["w_decay0"][0]), colz(inp["w_iclr0"][0]),
             colz(inp["k_k"][0]), colz(inp["k_a"][0]), colz(inp["r_k"][0]), colz(inp["ln_x_g"][0]),
             colz(inp["ln_x_b"][0]), colz(inp["conv_w"][0]), colz(inp["conv_b"][0]), colz(inp["lru_gx_b"][0]),
             colz(inp["lru_ga_b"][0]), colz(inp["lru_lambda"][0])]
    pv = np.ascontiguousarray(np.concatenate(parts, axis=1), dtype=np.float32)
    assert pv.shape == (128, NPV), pv.shape
    return pv


def kernel(**inp):
    if "nc" not in _CACHE:
        _CACHE["nc"] = build(PASSES)
    nc, stats = _CACHE["nc"]
    f = lambda a: np.ascontiguousarray(np.asarray(a, dtype=np.float32))
    xp = f(inp["x_prompt"])
    xsm = f(inp["x_sample"])
    pvec = _pvec(inp)
    shared = dict(
        pvec=pvec,
        w_in=f(np.asarray(inp["w_in"][0], np.float32).reshape(16, 128, INW // 128, 128).transpose(2, 1, 0, 3)),
        wdu=f(inp["w_decay_up"][0]), wiu=f(inp["w_iclr_up"][0]),
        worw=f(np.asarray(inp["w_out_rwkv"][0], np.float32).reshape(8, 128, 16, 128).transpose(2, 1, 0, 3)),
        wolru=f(np.asarray(inp["w_out_lru"][0], np.float32).reshape(8, 128, 16, 128).transpose(2, 1, 0, 3)),
        gxw=f(inp["lru_gx_w"][0]), gaw=f(inp["lru_ga_w"][0]),
        wout=f(np.asarray(inp["w_out"][0], np.float32).reshape(4, 4, 128, 4, 512).transpose(3, 0, 2, 1, 4)),
        fng=f(inp["final_norm_g"]).reshape(1, D))
    in_maps = []
    for c in range(NCORES):
        b, half = c // 2, c % 2
        m = dict(shared)
        m["xown"] = f(xp[b, half * 1024:(half + 1) * 1024])
        m["xprev"] = f(xp[b, 0:1024])
        m["flag"] = np.full((128, 1), float(half), np.float32)
        sl = slice(c * NS, (c + 1) * NS)
        m["xs"] = f(xsm[sl, 0])
        m["sshift"] = f(inp["state_shift"][0, sl])
        m["swkv"] = f(inp["state_wkv"][0, sl])
        m["sconv"] = f(inp["state_conv"][0, sl])
        m["slru"] = f(inp["state_lru"][0, sl])
        in_maps.append(m)
    res = run_bass_kernel_spmd(nc, in_maps, core_ids=list(range(NCORES)))
    R = res.results
    y_prompt = np.stack([np.concatenate([R[2 * b]["yp"], R[2 * b + 1]["yp"]], axis=0) for b in range(4)])
    y_sample = np.concatenate([R[c]["ys"] for c in range(NCORES)], axis=0)[:, None, :]
    odd = [R[2 * b + 1] for b in range(4)]
    nsh_p = np.stack([o["o_shp"].reshape(SHW) for o in odd])[None]
    nwkv_p = np.stack([o["o_wkvp"] for o in odd])[None]
    nconv_p = np.stack([o["o_convp"].reshape(3, 1024) for o in odd])[None]
    nlru_p = np.stack([o["o_lrup"].reshape(1024) for o in odd])[None]
    nsh_s = np.concatenate([R[c]["o_shs"] for c in range(NCORES)], axis=0)[None]
    nwkv_s = np.concatenate([R[c]["o_wkvs"] for c in range(NCORES)], axis=0)[None]
    nconv_s = np.concatenate([R[c]["o_convs"] for c in range(NCORES)], axis=0)[None]
    nlru_s = np.concatenate([R[c]["o_lrus"] for c in range(NCORES)], axis=0)[None]
    outs = (y_prompt, y_sample, nsh_p, nwkv_p, nconv_p, nlru_p, nsh_s, nwkv_s, nconv_s, nlru_s)
    return tuple(np.ascontiguousarray(o, dtype=np.float32) for o in outs)
```
